# Optimizing a Trainium2 kernel written in Bass

```python
import math
import jax, jax.numpy as jnp
from jax import lax
import numpy as np

D_MODEL = 1024
BATCH = 4
SEQ = 8192
DEPTH = 1

ATTN_HEADS = 8
HEAD_DIM = 64
ATTN_QK_WIDTH = ATTN_HEADS * 2 * HEAD_DIM
ATTN_V_WIDTH = ATTN_HEADS * 2 * HEAD_DIM
ATTN_WIDTH = ATTN_V_WIDTH
CONV_WIDTH = D_MODEL
CONV_K = 3
N_BRANCHES = 2
NUM_BUCKETS = 32
MAX_DISTANCE = 128
Q_BLOCK = 128
DEEPNORM_ALPHA = (2.0 * DEPTH) ** 0.25
DEEPNORM_BETA = (8.0 * DEPTH) ** -0.25
LN_EPS = 1e-5
RMS_EPS = 1e-5
NEG_INF = -1e30
IN_WIDTHS = [ATTN_QK_WIDTH, ATTN_QK_WIDTH, ATTN_V_WIDTH, ATTN_WIDTH,
             CONV_WIDTH, CONV_WIDTH, CONV_WIDTH, CONV_WIDTH]
IN_SPLITS = [int(s) for s in np.cumsum(IN_WIDTHS)[:-1]]
IN_TOTAL = int(sum(IN_WIDTHS))

kernel_name = "hybrid_diffattn_shortconv_gated_merge"


def layer_norm(x, gain=None, bias=None):
    xf = x.astype(jnp.float32)
    mu = jnp.mean(xf, axis=-1, keepdims=True)
    var = jnp.mean(jnp.square(xf - mu), axis=-1, keepdims=True)
    y = (xf - mu) * lax.rsqrt(var + LN_EPS)
    if gain is not None:
        y = y * gain.astype(jnp.float32) + bias.astype(jnp.float32)
    return y.astype(x.dtype)


def t5_bucket(rel):
    n = jnp.maximum(-rel, 0)
    max_exact = NUM_BUCKETS // 2
    is_small = n < max_exact
    nf = jnp.maximum(n, 1).astype(jnp.float32)
    large = max_exact + (jnp.log(nf / max_exact) / math.log(MAX_DISTANCE / max_exact)
                         * (NUM_BUCKETS - max_exact)).astype(jnp.int32)
    large = jnp.minimum(large, NUM_BUCKETS - 1)
    return jnp.where(is_small, n, large)


def diff_attention(q, k, v, lam, rel_bias):
    b_, s_, _ = q.shape
    nb = s_ // Q_BLOCK
    q = q * (HEAD_DIM ** -0.5)
    q = q.reshape(b_, nb, Q_BLOCK, ATTN_HEADS, 2, HEAD_DIM).transpose(1, 0, 3, 4, 2, 5)
    k = k.reshape(b_, s_, ATTN_HEADS, 2, HEAD_DIM).transpose(0, 2, 3, 1, 4)
    v = v.reshape(b_, s_, ATTN_HEADS, 2 * HEAD_DIM).transpose(0, 2, 1, 3)
    qpos = jnp.arange(s_, dtype=jnp.int32).reshape(nb, Q_BLOCK)
    kpos = jnp.arange(s_, dtype=jnp.int32)

    def block(args):
        q_blk, qp = args
        rel = kpos[None, :] - qp[:, None]
        bias = rel_bias[t5_bucket(rel)].astype(jnp.float32).transpose(2, 0, 1)
        s = jnp.einsum('bhiqd,bhikd->bhiqk', q_blk, k).astype(jnp.float32)
        s = s + bias[None, :, None]
        s = jnp.where((rel <= 0)[None, None, None], s, NEG_INF)
        p = jax.nn.softmax(s, axis=-1)
        a = p[:, :, 0] - lam * p[:, :, 1]
        return jnp.einsum('bhqk,bhkd->bhqd', a.astype(v.dtype), v)

    o = lax.map(block, (q, qpos))
    return o.transpose(1, 0, 3, 2, 4).reshape(b_, s_, ATTN_HEADS, 2 * HEAD_DIM)


def short_conv(bg, cg, xin, conv_w):
    v = cg * xin
    vp = jnp.pad(v, ((0, 0), (CONV_K - 1, 0), (0, 0)))
    s_ = v.shape[1]
    conv = conv_w[0] * vp[:, 0:s_] + conv_w[1] * vp[:, 1:s_ + 1] + conv_w[2] * vp[:, 2:s_ + 2]
    return bg * conv


def hybrid_layer(x, mod, w_in, lq1, lk1, lq2, lk2, subln_gain, rel_bias, conv_w,
                 w_branch, w_gate, b_gate, w_out, ln_gain, ln_bias, lambda_init):
    shift, scale, gate = jnp.split(mod, 3, axis=-1)
    u = layer_norm(x) * (1.0 + scale[:, None]) + shift[:, None]
    proj = jnp.einsum('bsd,de->bse', u, w_in)
    q, k, v, z_a, bg, cg, xin, z_c = jnp.split(proj, IN_SPLITS, axis=-1)

    lam = (jnp.exp(jnp.sum(lq1.astype(jnp.float32) * lk1.astype(jnp.float32)))
           - jnp.exp(jnp.sum(lq2.astype(jnp.float32) * lk2.astype(jnp.float32)))
           + lambda_init)
    o = diff_attention(q, k, v, lam, rel_bias).astype(jnp.float32)
    o = o * lax.rsqrt(jnp.mean(jnp.square(o), axis=-1, keepdims=True) + RMS_EPS)
    o = (o * subln_gain.astype(jnp.float32) * (1.0 - lambda_init)).astype(x.dtype)
    y_attn = o.reshape(x.shape[0], x.shape[1], ATTN_WIDTH) * jax.nn.silu(z_a)

    y_conv = short_conv(bg, cg, xin, conv_w) * jax.nn.silu(z_c)

    g_a, g_c = jnp.split(jax.nn.sigmoid(jnp.einsum('bsd,de->bse', u, w_gate) + b_gate), 2, axis=-1)
    merged = (g_a * jnp.einsum('bse,ed->bsd', y_attn, w_branch[0])
              + g_c * jnp.einsum('bse,ed->bsd', y_conv, w_branch[1]))
    out = jnp.einsum('bsd,de->bse', merged, w_out)

    return layer_norm(DEEPNORM_ALPHA * x + (1.0 + gate[:, None]) * out, ln_gain, ln_bias)


def setup_inputs(seed: int = 0) -> dict:
    key = jax.random.key(seed)
    ks = jax.random.split(key, 24)
    f32 = jnp.float32
    d = D_MODEL
    x = jax.random.normal(ks[0], (BATCH, SEQ, d), f32)
    c = jax.random.normal(ks[1], (BATCH, d), f32)
    w_ada = jax.random.normal(ks[2], (DEPTH, d, 3 * d), f32) * (d ** -0.5) * 0.1
    b_ada = jax.random.normal(ks[3], (DEPTH, 3 * d), f32) * 0.02
    col_scale = [1.0, 1.0, DEEPNORM_BETA, 1.0, 1.0, 1.0, DEEPNORM_BETA, 1.0]
    pieces = [jax.random.normal(kk, (DEPTH, d, wdt), f32) * (d ** -0.5) * sc
              for kk, wdt, sc in zip(jax.random.split(ks[4], len(IN_WIDTHS)), IN_WIDTHS, col_scale)]
    w_in = jnp.concatenate(pieces, axis=-1)
    lambda_q1 = jax.random.normal(ks[5], (DEPTH, HEAD_DIM), f32) * 0.1
    lambda_k1 = jax.random.normal(ks[6], (DEPTH, HEAD_DIM), f32) * 0.1
    lambda_q2 = jax.random.normal(ks[7], (DEPTH, HEAD_DIM), f32) * 0.1
    lambda_k2 = jax.random.normal(ks[8], (DEPTH, HEAD_DIM), f32) * 0.1
    subln_gain = 1.0 + 0.02 * jax.random.normal(ks[9], (DEPTH, 2 * HEAD_DIM), f32)
    rel_bias = jax.random.normal(ks[10], (NUM_BUCKETS, ATTN_HEADS), f32) * 0.3
    conv_w = jax.random.normal(ks[11], (DEPTH, CONV_K, CONV_WIDTH), f32) * (CONV_K ** -0.5)
    w_branch = jax.random.normal(ks[12], (DEPTH, N_BRANCHES, ATTN_WIDTH, d), f32) * (ATTN_WIDTH ** -0.5) * DEEPNORM_BETA
    w_gate = jax.random.normal(ks[13], (DEPTH, d, N_BRANCHES * d), f32) * (d ** -0.5)
    b_gate = jax.random.normal(ks[14], (DEPTH, N_BRANCHES * d), f32) * 0.1
    w_out = jax.random.normal(ks[15], (DEPTH, d, d), f32) * (d ** -0.5) * DEEPNORM_BETA
    ln_gain = 1.0 + 0.02 * jax.random.normal(ks[16], (DEPTH, d), f32)
    ln_bias = 0.02 * jax.random.normal(ks[17], (DEPTH, d), f32)
    return {"x": x, "c": c, "w_ada": w_ada, "b_ada": b_ada, "w_in": w_in,
            "lambda_q1": lambda_q1, "lambda_k1": lambda_k1, "lambda_q2": lambda_q2, "lambda_k2": lambda_k2,
            "subln_gain": subln_gain, "rel_bias": rel_bias, "conv_w": conv_w, "w_branch": w_branch,
            "w_gate": w_gate, "b_gate": b_gate, "w_out": w_out, "ln_gain": ln_gain, "ln_bias": ln_bias}


def reference(x, c, w_ada, b_ada, w_in, lambda_q1, lambda_k1, lambda_q2, lambda_k2,
              subln_gain, rel_bias, conv_w, w_branch, w_gate, b_gate, w_out, ln_gain, ln_bias):
    h = x
    for l in range(DEPTH):
        lambda_init = 0.8 - 0.6 * math.exp(-0.3 * l)
        mod = jnp.einsum('bd,de->be', c, w_ada[l]) + b_ada[l]
        h = hybrid_layer(h, mod, w_in[l], lambda_q1[l], lambda_k1[l], lambda_q2[l], lambda_k2[l],
                         subln_gain[l], rel_bias, conv_w[l], w_branch[l], w_gate[l], b_gate[l],
                         w_out[l], ln_gain[l], ln_bias[l], lambda_init)
    return h
```

```python
import math
from contextlib import ExitStack

import numpy as np
import concourse.bass as bass
import concourse.mybir as mybir
from concourse.bass_utils import run_bass_kernel_spmd

F32 = mybir.dt.float32
BF16 = mybir.dt.bfloat16
ALU = mybir.AluOpType
AF = mybir.ActivationFunctionType
AX = mybir.AxisListType

D = 1024
S = 8192
NCORES = 8
LN_EPS = 1e-5
RMS_EPS = 1e-5
LAMBDA_INIT = 0.8 - 0.6 * math.exp(0.0)
ALPHA = 2.0 ** 0.25
NF = 2048
TPW = 1536
COLTILE = False
QUAD = True


class _Op:
    __slots__ = ("eng", "fn", "dma", "deps", "need", "sem", "val", "prev", "idx", "phase")


class Sched:
    NP = 12

    def __init__(self, nc, es):
        self.nc = nc
        self.es = es
        self.sem = {}
        self.cnt = {}
        self.dpool = {}
        self.dval = {}
        self.dcnt = {}
        self.lastw = {}
        self.readers = {}
        self.waited = {}
        self.ops = []
        self.n = 0
        self.phase = 0

    def add(self, eng, fn, reads=(), writes=(), dma=False):
        op = _Op()
        op.eng, op.fn, op.dma = eng, fn, dma
        op.need, op.sem, op.val, op.prev = False, None, None, 0
        op.idx = self.n
        op.phase = self.phase
        self.n += 1
        cand = []
        wset = set(writes)
        for k in reads:
            if k in wset:
                continue
            w = self.lastw.get(k)
            if w is not None:
                cand.append(w)
        for k in wset:
            w = self.lastw.get(k)
            if w is not None:
                cand.append(w)
            cand.extend(self.readers.get(k, ()))
        latest = {}
        deps = []
        seen = set()
        for d in cand:
            if id(d) in seen:
                continue
            seen.add(id(d))
            if d.dma:
                deps.append(d)
                continue
            if d.eng == "pe" and eng == "pe" and not dma:
                continue
            if d.phase != self.phase:
                continue
            cur = latest.get(d.eng)
            if cur is None or d.idx > cur.idx:
                latest[d.eng] = d
        for d in latest.values():
            d.need = True
            deps.append(d)
        op.deps = deps
        for k in wset:
            self.lastw[k] = op
            self.readers[k] = []
        for k in reads:
            if k not in wset:
                self.readers.setdefault(k, []).append(op)
        self.ops.append(op)
        return op

    def dma(self, eng, out, in_, reads=(), writes=()):
        return self.add(eng, lambda e: e.dma_start(out=out, in_=in_), reads, writes, dma=True)

    def _pool(self, q):
        if q not in self.dpool:
            self.dpool[q] = [self.es.enter_context(self.nc.semaphore(f"d_{q}_{i}")) for i in range(self.NP)]
            self.dval[q] = [0] * self.NP
            self.dcnt[q] = 0
        return self.dpool[q]

    def run(self, name):
        nc = self.nc
        ops, self.ops = self.ops, []
        for op in ops:
            if op.dma:
                pool = self._pool(op.eng)
                i = self.dcnt[op.eng]
                self.dcnt[op.eng] += 1
                slot = i % self.NP
                op.sem = pool[slot]
                op.prev = self.dval[op.eng][slot]
                self.dval[op.eng][slot] += 16
                op.val = self.dval[op.eng][slot]
            elif op.need:
                if op.eng not in self.sem:
                    self.sem[op.eng] = self.es.enter_context(nc.semaphore(f"s_{op.eng}"))
                    self.cnt[op.eng] = 0
                self.cnt[op.eng] += 1
                op.sem = self.sem[op.eng]
                op.val = self.cnt[op.eng]
        by = {}
        for op in ops:
            by.setdefault(op.eng, []).append(op)
        sched = self

        def mk(e):
            def body(eng):
                waited = sched.waited.setdefault(e, {})
                for op in by.get(e, ()):
                    w = {}
                    for d in op.deps:
                        if d.val is None:
                            continue
                        k = id(d.sem)
                        if k not in w or w[k][1] < d.val:
                            w[k] = (d.sem, d.val)
                    if op.dma and op.prev > 0:
                        k = id(op.sem)
                        if k not in w or w[k][1] < op.prev:
                            w[k] = (op.sem, op.prev)
                    for k, (sem, val) in w.items():
                        if waited.get(k, 0) < val:
                            eng.wait_ge(sem, val)
                            waited[k] = val
                    ins = op.fn(eng)
                    if op.dma:
                        ins.then_inc(op.sem, 16)
                    elif op.need:
                        ins.then_inc(op.sem, 1)
                if e in sched.dpool:
                    for slot, sem in enumerate(sched.dpool[e]):
                        v = sched.dval[e][slot]
                        k = id(sem)
                        if v > 0 and waited.get(k, 0) < v:
                            eng.wait_ge(sem, v)
                            waited[k] = v
            return body

        with nc.Block() as block:
            block.tensor(mk("pe"))
            block.scalar(mk("act"))
            block.vector(mk("dve"))
            block.gpsimd(mk("pool"))
            block.sync(mk("sp"))
        self.phase += 1


def MM(out, lhsT, rhs, start=True, stop=True):
    return lambda e: e.matmul(out, lhsT, rhs, start=start, stop=stop)


def TR(out, in_, idn):
    return lambda e: e.transpose(out, in_, idn)


def ACTV(out, in_, func, bias=None, scale=None):
    kw = {}
    if bias is not None:
        kw["bias"] = bias
    if scale is not None:
        kw["scale"] = scale
    return lambda e: e.activation(out, in_, func, **kw)


def TT(out, in0, in1, op):
    return lambda e: e.tensor_tensor(out, in0, in1, op)


def TS(out, in0, s1, s2, op0, op1=None):
    if op1 is None:
        return lambda e: e.tensor_scalar(out, in0, s1, None, op0)
    return lambda e: e.tensor_scalar(out, in0, s1, s2, op0, op1)


def STT(out, in0, scalar, in1, op0, op1):
    return lambda e: e.scalar_tensor_tensor(out, in0, scalar, in1, op0, op1)


def CP(out, in_):
    return lambda e: e.tensor_copy(out, in_)


def MSET(out, v):
    return lambda e: e.memset(out, v)


def build_nc(stop_after=None):
    nc = bass.Bass("TRN2", target_bir_lowering=False)

    def din(name, shape, dt=F32):
        return nc.dram_tensor(name, list(shape), dt, kind="ExternalInput").ap()

    x_kv = din("x_kv", [S, D])
    x_own = din("x_own", [4096, D])
    x_halo = din("x_halo", [16, D])
    halo_mask = din("halo_mask", [128, 16])
    c_col = din("c_col", [128, 8])
    c_rep = din("c_rep", [128, 8, 128])
    w_ada = din("w_ada", [D, 3 * D])
    b_ada_col = din("b_ada_col", [128, 24])
    b_ada_g = din("b_ada_g", [128, D])
    w_in = din("w_in", [D, 8 * D])
    lamv = din("lamv", [128, 4, 64])
    gain_col = din("gain_col", [128, 1])
    rel_bias = din("rel_bias", [32, 8])
    rb31 = din("rb31", [8, 1])
    a_aug = din("a_aug", [33, NF])
    conv_col = din("conv_col", [128, 8, 3])
    w_branch = din("w_branch", [2, D, D])
    w_gate = din("w_gate", [D, 2 * D])
    b_gate_col = din("b_gate_col", [128, 16])
    w_out = din("w_out", [D, D])
    ln_gain_rep = din("ln_gain_rep", [128, D])
    ln_bias_rep = din("ln_bias_rep", [128, D])
    ident_in = din("ident_in", [128, 128])
    out = nc.dram_tensor("out", [4096, D], F32, kind="ExternalOutput").ap()

    dbg = stop_after is not None
    skind = "ExternalOutput" if dbg else "Internal"
    kT_scr = nc.dram_tensor("kT_scr", [8, 128, S], BF16, kind=skind).ap()
    v_scr = nc.dram_tensor("v_scr", [8, 128, 64, 128], BF16, kind=skind).ap()
    qT_scr = nc.dram_tensor("qT_scr", [8, 128, 4096], BF16, kind=skind).ap()
    uT_scr = nc.dram_tensor("uT_scr", [8, 128, 8, 512], BF16, kind=skind).ap()
    o_scr = nc.dram_tensor("o_scr", [8, 128, 8, 512], BF16, kind=skind).ap()
    ya_scr = nc.dram_tensor("ya_scr", [8, 128, 8, 512], BF16, kind=skind).ap()
    yc_scr = nc.dram_tensor("yc_scr", [8, 128, 8, 512], BF16, kind=skind).ap()
    fp_scr_t = nc.dram_tensor("fp_scr", [8, NF], F32, kind=skind)
    fp_scr = fp_scr_t.ap()
    dbg_t = nc.dram_tensor("dbg", [128, 2048], F32, kind=skind).ap()

    with ExitStack() as es:
        sc = Sched(nc, es)

        def sb(name, shape, dt=F32):
            return es.enter_context(nc.sbuf_tensor(name, list(shape), dt))

        PS = [es.enter_context(nc.psum_tensor(f"ps{i}", [128, 1024], F32)) for i in range(4)]
        PSB = [p.bitcast(BF16) for p in PS]

        def bk(i, h=None):
            if h is None:
                return [("ps", i, 0), ("ps", i, 1)]
            return [("ps", i, h)]

        def pv(b, n=512):
            return PS[b[0]][:, b[1] * 512: b[1] * 512 + n]

        psrot = [0]

        def next_bank():
            i = psrot[0] % 8
            psrot[0] += 1
            return (i // 2, i % 2)

        ident_f = sb("ident_f", [128, 128])
        ident = sb("ident", [128, 128], BF16)
        ones_bf = sb("ones_bf", [128, 128], BF16)
        ones_f = sb("ones_f", [128, 128])
        mhalf = sb("mhalf", [128, 512])
        modc = sb("modc", [128, 24])
        scale1 = sb("scale1", [128, 8])
        G1 = sb("G1", [128, D])
        lg_rep = sb("lg_rep", [128, D])
        lb_rep = sb("lb_rep", [128, D])
        neglam = sb("neglam", [128, 1])
        gain08 = sb("gain08", [128, 1])
        uTh = sb("uTh", [128, 8, 16], BF16)
        hmask = sb("hmask", [128, 16])
        convw = sb("convw", [128, 8, 3])
        bgate = sb("bgate", [128, 16])

        with ExitStack() as p0:
            def sb0(name, shape, dt=F32):
                return p0.enter_context(nc.sbuf_tensor(name, list(shape), dt))
            wada = sb0("wada", [128, 8, 3 * D])
            ccol = sb0("ccol", [128, 8])
            crep = sb0("crep", [128, 8, 128])
            bac = sb0("bac", [128, 24])
            bag = sb0("bag", [128, D])
            lv = sb0("lv", [128, 4, 64])
            lt = sb0("lt", [128, 2, 64])
            ls = sb0("ls", [128, 2])
            le = sb0("le", [128, 2])
            gcol = sb0("gcol", [128, 1])
            rba = sb0("rba", [33, 8])
            aug = sb0("aug", [33, NF])
            nrb = sb0("nrb", [8, 1])
            fps = sb0("fps", [8, NF])

            sc.dma("sp", ident_f[:], ident_in, writes=["ident_f"])
            sc.dma("sp", ccol[:], c_col, writes=["ccol"])
            sc.dma("sp", crep[:], c_rep, writes=["crep"])
            sc.dma("sp", bac[:], b_ada_col, writes=["bac"])
            sc.dma("sp", bag[:], b_ada_g, writes=["bag"])
            sc.dma("sp", lv[:], lamv, writes=["lv"])
            sc.dma("sp", gcol[:], gain_col, writes=["gcol"])
            sc.dma("sp", rba[0:32, :], rel_bias, writes=["rba0"])
            sc.dma("sp", nrb[:], rb31, writes=["nrb"])
            sc.dma("sp", aug[:], a_aug, writes=["aug"])
            sc.dma("sp", hmask[:], halo_mask, writes=["hmask"])
            sc.dma("sp", convw[:], conv_col, writes=["convw"])
            sc.dma("sp", bgate[:], b_gate_col, writes=["bgate"])
            sc.dma("sp", lg_rep[:], ln_gain_rep, writes=["lg_rep"])
            sc.dma("sp", lb_rep[:], ln_bias_rep, writes=["lb_rep"])
            wv_ = w_ada.rearrange("(k p) c -> p k c", p=128)
            for k in range(8):
                sc.dma("sp", wada[:, k, :], wv_[:, k, :], writes=[("wada", k)])

            sc.add("dve", CP(ident[:], ident_f[:]), reads=["ident_f"], writes=["ident"])
            sc.add("dve", MSET(ones_bf[:], 1.0), writes=["ones_bf"])
            sc.add("dve", MSET(ones_f[:], 1.0), writes=["ones_f"])
            sc.add("dve", MSET(mhalf[:], -0.5), writes=["mhalf"])
            sc.add("dve", MSET(rba[32:33, :], -30000.0), writes=["rba1"])
            sc.add("dve", TS(gain08[:], gcol[:], 1.0 - LAMBDA_INIT, None, ALU.mult), reads=["gcol"], writes=["gain08"])
            sc.add("dve", TS(nrb[:], nrb[:], -1.0, None, ALU.mult), writes=["nrb"])
            sc.add("dve", TT(lt[:, 0, :], lv[:, 0, :], lv[:, 1, :], ALU.mult), reads=["lv"], writes=["lt0"])
            sc.add("dve", TT(lt[:, 1, :], lv[:, 2, :], lv[:, 3, :], ALU.mult), reads=["lv"], writes=["lt1"])
            sc.add("dve", lambda e: e.reduce_sum(ls[:, 0:1], lt[:, 0, :], AX.X), reads=["lt0"], writes=["ls0"])
            sc.add("dve", lambda e: e.reduce_sum(ls[:, 1:2], lt[:, 1, :], AX.X), reads=["lt1"], writes=["ls1"])
            sc.add("act", ACTV(le[:], ls[:], AF.Exp), reads=["ls0", "ls1"], writes=["le"])
            sc.add("dve", STT(neglam[:], le[:, 1:2], -LAMBDA_INIT, le[:, 0:1], ALU.add, ALU.subtract),
                   reads=["le"], writes=["neglam"])
            for blk in range(24):
                for k in range(8):
                    sc.add("pe", MM(PS[0][:, blk:blk + 1], wada[:, k, blk * 128:(blk + 1) * 128], ccol[:, k:k + 1],
                                    start=(k == 0), stop=(k == 7)),
                           reads=[("wada", k), "ccol"], writes=bk(0, 0))
            sc.add("dve", TT(modc[:], PS[0][:, 0:24], bac[:], ALU.add), reads=["bac"], writes=bk(0, 0) + ["modc"])
            sc.add("dve", TS(scale1[:], modc[:, 8:16], 1.0, None, ALU.add), reads=["modc"], writes=["scale1"])
            for hf in range(2):
                for k in range(8):
                    sc.add("pe", MM(PS[1][:, hf * 512:(hf + 1) * 512], crep[:, k, :],
                                    wada[:, k, 2048 + hf * 512:2048 + (hf + 1) * 512], start=(k == 0), stop=(k == 7)),
                           reads=[("wada", k), "crep"], writes=bk(1, hf))
            sc.add("dve", TT(G1[:], PS[1][:], bag[:], ALU.add), reads=["bag"], writes=bk(1) + ["G1"])
            sc.add("dve", TS(G1[:], G1[:], 1.0, None, ALU.add), writes=["G1"])
            for cc in range(4):
                sc.add("pe", MM(PS[2 + cc // 2][0:8, (cc % 2) * 512:(cc % 2 + 1) * 512], rba[:, :],
                                aug[:, cc * 512:(cc + 1) * 512]),
                       reads=["rba0", "rba1", "aug"], writes=bk(2 + cc // 2, cc % 2))
            sc.add("act", ACTV(fps[:, 0:1024], PS[2][0:8, :], AF.Exp, bias=nrb[:, 0:1]),
                   reads=["nrb"], writes=bk(2) + ["fps0"])
            sc.add("act", ACTV(fps[:, 1024:2048], PS[3][0:8, :], AF.Exp, bias=nrb[:, 0:1]),
                   reads=["nrb"], writes=bk(3) + ["fps1"])
            sc.dma("sp", fp_scr, fps[:], reads=["fps0", "fps1"], writes=["fp_scr"])
            if stop_after == 0:
                sc.dma("sp", dbg_t[:, 0:24], modc[:], reads=["modc"], writes=["dbg0"])
                sc.dma("sp", dbg_t[:, 24:32], scale1[:], reads=["scale1"], writes=["dbg1"])
                sc.dma("sp", dbg_t[:, 1024:2048], G1[:], reads=["G1"], writes=["dbg3"])
            sc.run("p0")
        if stop_after == 0:
            return nc

        def load_weight_bf16(dst, dst_key, src_cols, ncols, stg, stg_key, slot_ctr, colblk=256):
            for c0 in range(0, ncols, colblk):
                s = slot_ctr[0] % 2
                slot_ctr[0] += 1
                src = src_cols(c0, c0 + colblk).rearrange("(k p) c -> p k c", p=128)
                sc.dma("sp", stg[s][:, :, 0:colblk], src, writes=[(stg_key, s)])
                if s == 0:
                    sc.add("act", ACTV(dst[:, :, c0:c0 + colblk], stg[s][:, :, 0:colblk], AF.Copy),
                           reads=[(stg_key, s)], writes=[(dst_key, c0 // colblk)])
                else:
                    sc.add("dve", CP(dst[:, :, c0:c0 + colblk], stg[s][:, :, 0:colblk]),
                           reads=[(stg_key, s)], writes=[(dst_key, c0 // colblk)])

        def wkeys(dst_key, c0, c1, colblk=256):
            return [(dst_key, i) for i in range(c0 // colblk, (c1 - 1) // colblk + 1)]

        ev = [0]

        def evac(out_ap, in_ap, reads, writes, scale=None, bias=None):
            ev[0] += 1
            if ev[0] % 2 == 0:
                if scale is None:
                    sc.add("dve", CP(out_ap, in_ap), reads=reads, writes=writes)
                elif bias is None:
                    sc.add("dve", TS(out_ap, in_ap, scale, None, ALU.mult), reads=reads, writes=writes)
                else:
                    sc.add("dve", TS(out_ap, in_ap, scale, bias, ALU.mult, ALU.add), reads=reads, writes=writes)
            else:
                if scale is None:
                    sc.add("act", ACTV(out_ap, in_ap, AF.Copy), reads=reads, writes=writes)
                elif bias is None:
                    sc.add("act", ACTV(out_ap, in_ap, AF.Copy, scale=scale), reads=reads, writes=writes)
                else:
                    sc.add("act", ACTV(out_ap, in_ap, AF.Identity, bias=bias, scale=scale), reads=reads, writes=writes)

        with ExitStack() as pa:
            def sba(name, shape, dt=F32):
                return pa.enter_context(nc.sbuf_tensor(name, list(shape), dt))
            wq = sba("wq", [128, 8, 1024], BF16)
            wk = sba("wk", [128, 8, 1024], BF16)
            wv = sba("wv", [128, 8, 1024], BF16)
            stg = [sba(f"stgA{i}", [128, 8, 256]) for i in range(2)]
            xt = [sba(f"xt{i}", [128, 4, D]) for i in range(2)]
            xn = [sba(f"xn{i}", [128, 4, D], BF16) for i in range(2)]
            uT = [sba(f"uT{i}", [128, 8, 512], BF16) for i in range(2)]
            ksb = [sba(f"ksb{i}", [128, 8, 512], BF16) for i in range(2)]
            vsb = [sba(f"vsb{i}", [128, 4, D], BF16) for i in range(2)]
            stats = [sba(f"stats{i}", [128, 4, 12]) for i in range(2)]
            mv = [sba(f"mv{i}", [128, 4, 2]) for i in range(2)]
            rstd = [sba(f"rstd{i}", [128, 4]) for i in range(2)]
            xh = sba("xh", [16, D])
            xhn = sba("xhn", [16, D], BF16)
            sth = sba("sth", [16, 12])
            mvh = sba("mvh", [16, 2])
            rsh = sba("rsh", [16, 1])

            def ld_x(src_rows, s):
                sc.dma("sp", xt[s][:], src_rows.rearrange("(a p) d -> p a d", p=128), writes=[("xt", s)])

            def ln_stats(s):
                for a in range(4):
                    for hh in range(2):
                        sc.add("dve", (lambda o_, i_: lambda e: e.bn_stats(o_, i_))(
                            stats[s][:, a, hh * 6:(hh + 1) * 6], xt[s][:, a, hh * 512:(hh + 1) * 512]),
                            reads=[("xt", s)], writes=[("stats", s, a, hh)])
                    sc.add("dve", (lambda o_, i_: lambda e: e.bn_aggr(o_, i_))(mv[s][:, a, :], stats[s][:, a, :]),
                           reads=[("stats", s, a, 0), ("stats", s, a, 1)], writes=[("mv", s, a)])
                sc.add("dve", TS(rstd[s][:], mv[s][:, :, 1], LN_EPS, None, ALU.add),
                       reads=[("mv", s, a) for a in range(4)], writes=[("rstd", s)])
                sc.add("pool", TT(rstd[s][:], rstd[s][:], mhalf[:, 0:4], ALU.pow), reads=["mhalf"], writes=[("rstd", s)])
                for a in range(4):
                    sc.add("dve", TS(xn[s][:, a, :], xt[s][:, a, :], mv[s][:, a, 0:1], rstd[s][:, a:a + 1],
                                     ALU.subtract, ALU.mult),
                           reads=[("xt", s), ("mv", s, a), ("rstd", s)], writes=[("xn", s, a)])

            def ln_uT(s):
                for dmc in range(8):
                    b = next_bank()
                    for a in range(4):
                        sc.add("pe", TR(PSB[b[0]][:, b[1] * 1024 + a * 128: b[1] * 1024 + (a + 1) * 128],
                                        xn[s][:, a, dmc * 128:(dmc + 1) * 128], ident[:]),
                               reads=[("xn", s, a), "ident"], writes=bk(*b))
                    evac(uT[s][:, dmc, :], PSB[b[0]][:, b[1] * 1024: b[1] * 1024 + 512],
                         reads=["scale1", "modc"], writes=bk(*b) + [("uT", s, dmc)],
                         scale=scale1[:, dmc:dmc + 1], bias=modc[:, dmc:dmc + 1])

            def proj_fm(w, wkey, s, dst, dst_key, scale=None):
                for m in range(8):
                    pump_wa(1)
                    b = next_bank()
                    for k in range(8):
                        sc.add("pe", MM(pv(b), w[:, k, m * 128:(m + 1) * 128], uT[s][:, k, :],
                                        start=(k == 0), stop=(k == 7)),
                               reads=[("uT", s, k)] + wkeys(wkey, m * 128, (m + 1) * 128), writes=bk(*b))
                    evac(dst[s][:, m, :], pv(b), reads=[], writes=bk(*b) + [(dst_key, s, m)], scale=scale)

            srcs = [x_kv[t * 512:(t + 1) * 512, :] for t in range(16)] + [x_own[j * 512:(j + 1) * 512, :] for j in range(8)]
            ld_x(srcs[0], 0)
            ctr = [0]
            wq_a = ([(wk, "wk", 1024, c0) for c0 in range(0, 1024, 256)] + [(wv, "wv", 2048, c0) for c0 in range(0, 1024, 256)]
                    + [(wq, "wq", 0, c0) for c0 in range(0, 1024, 256)])

            def pump_wa(n):
                for _ in range(n):
                    if wq_a:
                        dst_, key_, base_, c0_ = wq_a.pop(0)
                        s_ = ctr[0] % 2
                        ctr[0] += 1
                        sc.dma("sp", stg[s_][:], w_in[:, base_ + c0_:base_ + c0_ + 256].rearrange("(k p) c -> p k c", p=128),
                               writes=[("stgA", s_)])
                        if s_ == 0:
                            sc.add("act", ACTV(dst_[:, :, c0_:c0_ + 256], stg[s_][:], AF.Copy), reads=[("stgA", s_)],
                                   writes=[(key_, c0_ // 256)])
                        else:
                            sc.add("dve", CP(dst_[:, :, c0_:c0_ + 256], stg[s_][:]), reads=[("stgA", s_)],
                                   writes=[(key_, c0_ // 256)])
            pump_wa(2)
            ld_x(srcs[1], 1)
            ln_stats(0)
            for it in range(24):
                s = it % 2
                ln_uT(s)
                if it + 1 < 24:
                    ln_stats((it + 1) % 2)
                if it + 2 < 24:
                    ld_x(srcs[it + 2], s)
                if it < 16:
                    t = it
                    proj_fm(wk, "wk", s, ksb, "ksb")
                    sc.dma("sp", kT_scr[:, :, t * 512:(t + 1) * 512].rearrange("m p n -> p m n"), ksb[s][:],
                           reads=[("ksb", s, m) for m in range(8)], writes=[("kT_scr", t)])
                    for a in range(4):
                        for cn in range(2):
                            b = next_bank()
                            for k in range(8):
                                sc.add("pe", MM(pv(b), uT[s][:, k, a * 128:(a + 1) * 128],
                                                wv[:, k, cn * 512:(cn + 1) * 512], start=(k == 0), stop=(k == 7)),
                                       reads=[("uT", s, k)] + wkeys("wv", cn * 512, (cn + 1) * 512), writes=bk(*b))
                            evac(vsb[s][:, a, cn * 512:(cn + 1) * 512], pv(b), reads=[],
                                 writes=bk(*b) + [("vsb", s, a, cn)])
                        sc.dma("sp", v_scr[:, :, t * 4 + a, :].rearrange("h p d -> p h d"),
                               vsb[s][:, a, :].rearrange("p (h d) -> p h d", h=8),
                               reads=[("vsb", s, a, cn) for cn in range(2)], writes=[("v_scr", t, a)])
                else:
                    j = it - 16
                    sc.dma("sp", uT_scr[j], uT[s][:], reads=[("uT", s, d) for d in range(8)], writes=[("uT_scr", j)])
                    proj_fm(wq, "wq", s, ksb, "ksb", scale=0.125)
                    sc.dma("sp", qT_scr[:, :, j * 512:(j + 1) * 512].rearrange("m p n -> p m n"), ksb[s][:],
                           reads=[("ksb", s, m) for m in range(8)], writes=[("qT_scr", j)])
            sc.dma("sp", xh[:], x_halo, writes=["xh"])
            for hh in range(2):
                sc.add("dve", (lambda o_, i_: lambda e: e.bn_stats(o_, i_))(
                    sth[:, hh * 6:(hh + 1) * 6], xh[:, hh * 512:(hh + 1) * 512]),
                    reads=["xh"], writes=[("sth", hh)])
            sc.add("dve", lambda e: e.bn_aggr(mvh[:], sth[:]), reads=[("sth", 0), ("sth", 1)], writes=["mvh"])
            sc.add("dve", TS(rsh[:], mvh[:, 1:2], LN_EPS, None, ALU.add), reads=["mvh"], writes=["rsh"])
            sc.add("pool", TT(rsh[:], rsh[:], mhalf[0:16, 0:1], ALU.pow), reads=["mhalf"], writes=["rsh"])
            sc.add("dve", TS(xhn[:], xh[:], mvh[:, 0:1], rsh[:, 0:1], ALU.subtract, ALU.mult),
                   reads=["xh", "mvh", "rsh"], writes=["xhn"])
            for dmc in range(8):
                b = next_bank()
                sc.add("pe", TR(PSB[b[0]][:, b[1] * 1024: b[1] * 1024 + 16], xhn[:, dmc * 128:(dmc + 1) * 128],
                                ident[0:16, 0:16]),
                       reads=["xhn", "ident"], writes=bk(*b))
                evac(uTh[:, dmc, :], PSB[b[0]][:, b[1] * 1024: b[1] * 1024 + 16], reads=["scale1", "modc"],
                     writes=bk(*b) + [("uTh", dmc)], scale=scale1[:, dmc:dmc + 1], bias=modc[:, dmc:dmc + 1])
            sc.run("pA")
        if stop_after == 1:
            return nc

        with ExitStack() as pb:
            def sbb(name, shape, dt=F32):
                return pb.enter_context(nc.sbuf_tensor(name, list(shape), dt))
            KT = [sbb(f"KT{i}", [128, S], BF16) for i in range(2)]
            VT = [sbb(f"VT{i}", [128, 64, 128], BF16) for i in range(2)]
            QT = [sbb(f"QT{i}", [128, 4096], BF16) for i in range(2)]
            TP = [sbb(f"TP{i}", [128, TPW]) for i in range(2)]
            TPb = [sbb(f"TPb{i}", [128, TPW], BF16) for i in range(2)]
            NPB = 6
            PB = [sbb(f"PB{i}", [128, 1024], BF16) for i in range(NPB)]
            PF = [sbb(f"PF{i}", [128, 1024], BF16) for i in range(2)]
            OE = [sbb(f"OE{i}", [128, 1024]) for i in range(2)]
            LE = [sbb(f"LE{i}", [64, 512] if COLTILE else [128, 1024]) for i in range(2)]
            OS = [sbb(f"OS{i}", [128, 512]) for i in range(2)]
            SQ = [sbb(f"SQ{i}", [128, 512]) for i in range(2)]
            MS = sbb("MS", [128, 512])
            LL = [sbb(f"LL{i}", [128, 512]) for i in range(2)]
            ON = [sbb(f"ON{i}", [128, 512], BF16) for i in range(2)]
            lneps = sbb("lneps", [128, 1])
            sc.add("dve", MSET(lneps[:], -0.5 * math.log(RMS_EPS)), writes=["lneps"])
            sel1 = sbb("sel1", [64, 128])
            sel2 = sbb("sel2", [64, 128])
            sc.add("dve", MSET(sel1[:], 0.0), writes=["sel1"])
            sc.add("dve", MSET(sel2[:], 0.0), writes=["sel2"])
            sc.add("dve", MSET(sel1[0:1, :], 1.0), writes=["sel1"])
            sc.add("dve", MSET(sel2[32:33, :], 1.0), writes=["sel2"])

            pbc = [0]
            pfc = [0]
            onc = [0]
            qst = [0, 0]
            lpend = []
            NQS = 4
            QS = [sbb(f"QS{i}", [128, 1024], BF16) for i in range(NQS)]

            def load_head(h):
                hs = h % 2
                sc.dma("sp", KT[hs][:], kT_scr[h], reads=[("kT_scr", t) for t in range(16)], writes=[("KT", hs)])
                sc.dma("sp", VT[hs][:], v_scr[h], reads=[("v_scr", t, a) for t in range(16) for a in range(4)],
                       writes=[("VT", hs)])
                sc.dma("sp", QT[hs][:], qT_scr[h], reads=[("qT_scr", j) for j in range(8)], writes=[("QT", hs)])
                sc.dma("sp", TP[hs][:], bass.AP(fp_scr_t, h * NF, [[1, 128], [1, TPW]]),
                       reads=["fp_scr"], writes=[("TP", hs)])
                sc.add("pool", CP(TPb[hs][:], TP[hs][:]), reads=[("TP", hs)], writes=[("TPb", hs)])

            items = []
            fifo = []
            cc = 0
            for h in range(8):
                for j in range(8):
                    e = cc % 2
                    cc += 1
                    for kb in range(8 * j + 8):
                        items.append(("u", h, j, kb, e))
                        while fifo and fifo[0][0] <= len(items):
                            items.append(fifo.pop(0)[1])
                    for n_, nm in ((2, "le"), (4, "g1"), (6, "g2"), (8, "g3"), (10, "p1"), (12, "p2"), (14, "p3"),
                                   (16, "p4"), (18, "p5"), (22, "f3"), (24, "f4"), (26, "f5"), (28, "f6")):
                        fifo.append((len(items) + n_, (nm, h, j, e)))
                    fifo.sort(key=lambda t_: t_[0])
            for f_ in fifo:
                items.append(f_[1])
                items.append(("nop",))
                items.append(("nop",))

            slot_of = {}
            kcnt = 0
            for i_, it_ in enumerate(items):
                if it_[0] in ("u", "f3"):
                    slot_of[i_] = kcnt % 2
                    kcnt += 1

            def stage1(idx):
                it = items[idx]
                s = slot_of.get(idx, 0)
                if it[0] == "u":
                    _, h, j, kb, e = it
                    hs = h % 2
                    qs = slice(j * 512, (j + 1) * 512)
                    ks = slice(kb * 128, (kb + 1) * 128)
                    sc.add("pe", MM(PS[s][:, 0:512], KT[hs][0:64, ks], QT[hs][0:64, qs]),
                           reads=[("KT", hs), ("QT", hs)], writes=bk(s, 0))
                    sc.add("pe", MM(PS[s][:, 512:1024], KT[hs][64:128, ks], QT[hs][64:128, qs]),
                           reads=[("KT", hs), ("QT", hs)], writes=bk(s, 1))
                elif it[0] == "f1":
                    e = it[3]
                    pass
                elif it[0] == "f2":
                    e = it[3]
                    pass
                elif it[0] == "f3":
                    e = it[3]
                    sc.add("pe", MM(PS[s][:, 0:512], ones_f[:], SQ[e][:]), reads=[("SQ", e), "ones_f"], writes=bk(s, 0))

            ust = {}

            def stage2a(idx):
                it = items[idx]
                s = slot_of.get(idx, 0)
                if it[0] == "f3":
                    e = it[3]
                    sc.add("dve", TS(MS[:], PS[s][:, 0:512], 1.0 / 128.0, RMS_EPS, ALU.mult, ALU.add),
                           reads=[], writes=bk(s, 0) + ["MS"])
                    return
                if it[0] != "u":
                    return
                _, h, j, kb, e = it
                hs = h % 2
                if j == 4 and kb == 0 and h + 1 < 8:
                    load_head(h + 1)
                p = pbc[0] % NPB
                pbc[0] += 1
                ust[idx] = p
                pk = [("PB", p, 0), ("PB", p, 1)]
                if kb < 8 * j - 1:
                    sc.add("act", ACTV(PB[p][:], PS[s][:], AF.Exp), reads=[], writes=bk(s) + pk)
                else:
                    f = pfc[0] % 2
                    pfc[0] += 1
                    off = 128 * (8 * j + 7 - kb)
                    for m in range(2):
                        ms_ = slice(m * 512, (m + 1) * 512)
                        sc.add("act", ACTV(PF[f][:, ms_], PS[s][:, ms_], AF.Exp), reads=[],
                               writes=bk(s, m) + [("PF", f, m)])
                        sc.add("dve", TT(PB[p][:, ms_], PF[f][:, ms_], TPb[hs][:, off:off + 512], ALU.mult),
                               reads=[("PF", f, m), ("TPb", hs)], writes=[pk[m]])

            def stage2b(idx):
                it = items[idx]
                s = slot_of.get(idx, 0)
                if it[0] == "g1":
                    e = it[3]
                    sc.add("dve", TT(LL[e][:], LE[e][:, 0:512], LE[e][:, 512:1024], ALU.mult),
                           reads=[("LE", e)], writes=[("LL", e)])
                    return
                if it[0] in ("g2", "g3"):
                    e = it[3]
                    cs_ = slice(0, 256) if it[0] == "g2" else slice(256, 512)
                    sc.add("dve", (lambda o_: lambda en: en.reciprocal(o_, o_))(LL[e][:, cs_]), reads=[], writes=[("LL", e)])
                    return
                if it[0] == "p1":
                    e = it[3]
                    sc.add("dve", TT(OE[e][:, 0:512], OE[e][:, 0:512], LE[e][:, 512:1024], ALU.mult),
                           reads=[("LE", e)], writes=[("OE", e)])
                    return
                if it[0] == "p2":
                    e = it[3]
                    sc.add("dve", TT(OE[e][:, 512:1024], OE[e][:, 512:1024], LE[e][:, 0:512], ALU.mult),
                           reads=[("LE", e)], writes=[("OE", e)])
                    return
                if it[0] == "p3":
                    e = it[3]
                    sc.add("dve", TT(OS[e][:], OE[e][:, 0:512], OE[e][:, 512:1024], ALU.add),
                           reads=[("OE", e)], writes=[("OS", e)])
                    return
                if it[0] == "p4":
                    e = it[3]
                    sc.add("dve", TT(OS[e][:], OS[e][:], LL[e][:], ALU.mult), reads=[("LL", e)], writes=[("OS", e)])
                    return
                if it[0] == "p5":
                    e = it[3]
                    sc.add("dve", TT(SQ[e][:], OS[e][:], OS[e][:], ALU.mult), reads=[("OS", e)], writes=[("SQ", e)])
                    return
                if it[0] == "nop":
                    return
                if it[0] == "le":
                    e = it[3]
                    sc.add("act", ACTV(LE[e][:], PS[3][:], AF.Copy), reads=[], writes=bk(3) + [("LE", e)])
                    return
                if it[0] == "f3":
                    return
                if it[0] == "f4":
                    sc.add("act", ACTV(MS[:], MS[:], AF.Ln), reads=[], writes=["MS"])
                    return
                if it[0] == "f5":
                    sc.add("act", ACTV(MS[:], MS[:], AF.Exp, scale=-0.5), reads=[], writes=["MS"])
                    return
                if it[0] == "f6":
                    _, h, j, e = it
                    o = onc[0] % 2
                    onc[0] += 1
                    sc.add("dve", STT(ON[o][:], OS[e][:], gain08[:, 0:1], MS[:], ALU.mult, ALU.mult),
                           reads=[("OS", e), "MS", "gain08"], writes=[("ON", o)])
                    sc.dma("sp", o_scr[j][:, h, :], ON[o][:], reads=[("ON", o)], writes=[("o_scr", j, h)])
                    return
                _, h, j, kb, e = it
                hs = h % 2
                nkb = 8 * j + 8
                p = ust.pop(idx)
                pk = [("PB", p, 0), ("PB", p, 1)]
                st, sp_ = (kb == 0), (kb == nkb - 1)
                for m in range(2):
                    sc.add("pe", MM(PS[2][:, m * 512:(m + 1) * 512], VT[hs][:, kb, :],
                                    PB[p][:, m * 512:(m + 1) * 512], start=st, stop=sp_),
                           reads=[pk[m], ("VT", hs)], writes=bk(2, m))
                r = kb % 4
                if r == 0:
                    qst[0] = p
                    qst[1] += 1
                q = qst[1] % NQS

                def emit_L(q_, st_, sp__):
                    for m in range(2):
                        sc.add("pe", MM(PS[3][:, m * 512:(m + 1) * 512], ones_bf[:],
                                        QS[q_][:, m * 512:(m + 1) * 512], start=st_, stop=sp__),
                               reads=[("QS", q_), "ones_bf"], writes=bk(3, m))
                if r == 1 and lpend:
                    emit_L(*lpend.pop())
                if r == 1:
                    sc.add("dve", TT(QS[q][:], PB[qst[0]][:], PB[p][:], ALU.add),
                           reads=[("PB", qst[0], 0), ("PB", qst[0], 1)] + pk, writes=[("QS", q)])
                elif r > 1:
                    sc.add("dve", TT(QS[q][:], QS[q][:], PB[p][:], ALU.add), reads=pk, writes=[("QS", q)])
                if r == 3:
                    if sp_:
                        emit_L(q, kb == 3, True)
                    else:
                        lpend.append((q, kb == 3, False))
                if kb == nkb - 1:
                    sc.add("dve", CP(OE[e][:, 0:512], PS[2][:, 0:512]), reads=[], writes=bk(2, 0) + [("OE", e)])
                    sc.add("dve", TS(OE[e][:, 512:1024], PS[2][:, 512:1024], neglam[:, 0:1], None, ALU.mult),
                           reads=["neglam"], writes=bk(2, 1) + [("OE", e)])

            load_head(0)
            stage1(0)
            stage1(1)
            for idx in range(len(items)):
                stage2a(idx)
                if idx + 2 < len(items):
                    stage1(idx + 2)
                stage2b(idx)
            sc.run("pB")
        if stop_after == 2:
            return nc

        with ExitStack() as pc:
            def sbc(name, shape, dt=F32):
                return pc.enter_context(nc.sbuf_tensor(name, list(shape), dt))
            wc = sbc("wc", [128, 8, 5120], BF16)
            stg = [sbc(f"stgC{i}", [128, 8, 256]) for i in range(2)]
            uT = [sbc(f"uTc{i}", [128, 8, 512], BF16) for i in range(2)]
            oT = [sbc(f"oTc{i}", [128, 8, 512], BF16) for i in range(2)]
            ya = [sbc(f"yac{i}", [128, 8, 512], BF16) for i in range(2)]
            yc = [sbc(f"ycc{i}", [128, 8, 512], BF16) for i in range(2)]
            vh = sbc("vh", [128, 8, 16])
            za = [sbc(f"za{i}", [128, 512]) for i in range(2)]
            Csb = [sbc(f"Csb{i}", [128, 512]) for i in range(2)]
            vv = [sbc(f"vv{i}", [128, 514]) for i in range(2)]
            tt = [sbc(f"tt{i}", [128, 512]) for i in range(2)]
            sz = [sbc(f"sz{i}", [128, 512]) for i in range(2)]
            t2 = [sbc(f"t2{i}", [128, 512]) for i in range(2)]

            def mm8(b, wcol0, rhs_list, rkeys, n=512):
                for k in range(8):
                    sc.add("pe", MM(pv(b, n), wc[:, k, wcol0:wcol0 + 128], rhs_list[k], start=(k == 0), stop=(k == 7)),
                           reads=rkeys + wkeys("wc", wcol0, wcol0 + 128), writes=bk(*b))

            hk = [("uTh", d) for d in range(8)]
            hl = [uTh[:, k, :] for k in range(8)]

            def halo(cb):
                b1 = next_bank()
                mm8(b1, 2048 + cb * 128, hl, hk, n=16)
                b2 = next_bank()
                mm8(b2, 3072 + cb * 128, hl, hk, n=16)
                sc.add("dve", TT(vh[:, cb, :], pv(b1, 16), hmask[:], ALU.mult), reads=["hmask"],
                       writes=bk(*b1) + [("vh", cb)])
                sc.add("dve", TT(vh[:, cb, :], pv(b2, 16), vh[:, cb, :], ALU.mult), reads=[],
                       writes=bk(*b2) + [("vh", cb)])

            def ld_c1b(j):
                s = j % 2
                sc.dma("sp", uT[s][:], uT_scr[j], reads=[("uT_scr", j)], writes=[("uTc", s)])

            def ld_c1(j):
                s = j % 2
                sc.dma("sp", uT[s][:], uT_scr[j], reads=[("uT_scr", j)], writes=[("uTc", s)])
                sc.dma("sp", oT[s][:], o_scr[j], reads=[("o_scr", j, h) for h in range(8)], writes=[("oTc", s)])

            tc_ = [0]
            ld_c1(0)
            ctr = [0]

            def load_wc(c0):
                s_ = ctr[0] % 2
                ctr[0] += 1
                src = w_in[:, 3072 + c0:3072 + c0 + 256].rearrange("(k p) c -> p k c", p=128)
                sc.dma("sp", stg[s_][:], src, writes=[("stgC", s_)])
                if s_ == 0:
                    sc.add("act", ACTV(wc[:, :, c0:c0 + 256], stg[s_][:], AF.Copy), reads=[("stgC", s_)],
                           writes=[("wc", c0 // 256)])
                else:
                    sc.add("dve", CP(wc[:, :, c0:c0 + 256], stg[s_][:]), reads=[("stgC", s_)], writes=[("wc", c0 // 256)])
            wq_c1 = [c0 for c0 in range(0, 1024, 256)] + [base + pr * 256 for pr in range(4)
                                                           for base in (1024, 2048, 3072, 4096)]

            def pump_wc(n):
                for _ in range(n):
                    if wq_c1:
                        load_wc(wq_c1.pop(0))
            pump_wc(2)

            for j in range(8):
                s = j % 2
                if j + 1 < 8:
                    ld_c1(j + 1)
                ukeys = [("uTc", s)]
                ul = [uT[s][:, k, :] for k in range(8)]
                for m in range(8):
                    i = tc_[0] % 2
                    tc_[0] += 1
                    if j == 0 and m % 2 == 0:
                        pump_wc(1)
                    if j > 0 and m in (0, 4):
                        pump_wc(1)
                    b = next_bank()
                    mm8(b, m * 128, ul, ukeys)
                    sc.add("act", ACTV(za[i][:], pv(b), AF.Silu), reads=[], writes=bk(*b) + [("za", i)])
                    sc.add("pool", TT(ya[s][:, m, :], za[i][:], oT[s][:, m, :], ALU.mult),
                           reads=[("za", i), ("oTc", s)], writes=[("yac", s, m)])
                sc.dma("sp", ya_scr[j], ya[s][:], reads=[("yac", s, m) for m in range(8)], writes=[("ya_scr", j)])
            pump_wc(100)
            ld_c1b(0)
            for j in range(8):
                s = j % 2
                if j + 1 < 8:
                    ld_c1b(j + 1)
                ukeys = [("uTc", s)]
                ul = [uT[s][:, k, :] for k in range(8)]
                for cb in range(8):
                    i = tc_[0] % 2
                    tc_[0] += 1
                    if j == 0:
                        halo(cb)
                    bB = next_bank()
                    mm8(bB, 1024 + cb * 128, ul, ukeys)
                    bC = next_bank()
                    mm8(bC, 2048 + cb * 128, ul, ukeys)
                    bX = next_bank()
                    mm8(bX, 3072 + cb * 128, ul, ukeys)
                    bZ = next_bank()
                    mm8(bZ, 4096 + cb * 128, ul, ukeys)
                    sc.add("act", ACTV(Csb[i][:], pv(bC), AF.Copy), reads=[], writes=bk(*bC) + [("Csb", i)])
                    sc.add("act", ACTV(sz[i][:], pv(bZ), AF.Silu), reads=[], writes=bk(*bZ) + [("sz", i)])
                    sc.add("dve", TT(vv[i][:, 2:514], pv(bX), Csb[i][:], ALU.mult), reads=[("Csb", i)],
                           writes=bk(*bX) + [("vv", i)])
                    sc.add("pool", CP(vv[i][:, 0:2], vh[:, cb, 2 * j:2 * j + 2]), reads=[("vh", cb)],
                           writes=[("vvh", i)])
                    rk = [("vv", i), ("vvh", i), "convw"]
                    sc.add("dve", TS(tt[i][:], vv[i][:, 0:512], convw[:, cb, 0:1], None, ALU.mult),
                           reads=rk, writes=[("tt", i)])
                    sc.add("dve", STT(tt[i][:], vv[i][:, 1:513], convw[:, cb, 1:2], tt[i][:], ALU.mult, ALU.add),
                           reads=rk, writes=[("tt", i)])
                    sc.add("dve", STT(tt[i][:], vv[i][:, 2:514], convw[:, cb, 2:3], tt[i][:], ALU.mult, ALU.add),
                           reads=rk, writes=[("tt", i)])
                    sc.add("dve", TT(t2[i][:], pv(bB), tt[i][:], ALU.mult), reads=[("tt", i)],
                           writes=bk(*bB) + [("t2", i)])
                    sc.add("pool", TT(yc[s][:, cb, :], t2[i][:], sz[i][:], ALU.mult), reads=[("t2", i), ("sz", i)],
                           writes=[("ycc", s, cb)])
                sc.dma("sp", yc_scr[j], yc[s][:], reads=[("ycc", s, cb) for cb in range(8)], writes=[("yc_scr", j)])
            sc.run("pC1")
        if stop_after == 3:
            return nc

        with ExitStack() as pd:
            def sbd(name, shape, dt=F32):
                return pd.enter_context(nc.sbuf_tensor(name, list(shape), dt))
            wg = sbd("wg", [128, 8, 2048], BF16)
            wb0 = sbd("wb0", [128, 8, 1024], BF16)
            wb1 = sbd("wb1", [128, 8, 1024], BF16)
            wo = sbd("wo", [128, 8, 1024], BF16)
            stg = [sbd(f"stgD{i}", [128, 8, 256]) for i in range(2)]
            uT = [sbd(f"uTd{i}", [128, 8, 512], BF16) for i in range(2)]
            ya = [sbd(f"yad{i}", [128, 8, 512], BF16) for i in range(2)]
            yc = [sbd(f"ycd{i}", [128, 8, 512], BF16) for i in range(2)]
            xo = [sbd(f"xo{i}", [128, D]) for i in range(2)]
            mT = sbd("mT", [128, 8, 512], BF16)
            ga = [sbd(f"ga{i}", [128, 512]) for i in range(2)]
            gc = [sbd(f"gc{i}", [128, 512]) for i in range(2)]
            t1 = [sbd(f"t1{i}", [128, 512]) for i in range(2)]
            t2 = [sbd(f"t2d{i}", [128, 512]) for i in range(2)]
            rr = [sbd(f"rr{i}", [128, D]) for i in range(2)]
            yy = rr
            st2 = [sbd(f"st2{i}", [128, 12]) for i in range(2)]
            mv2 = [sbd(f"mv2{i}", [128, 2]) for i in range(2)]
            rs2 = [sbd(f"rs2{i}", [128, 1]) for i in range(2)]

            def mmw(b, w, wkey, wcol0, rhs_list, rkeys):
                for k in range(8):
                    sc.add("pe", MM(pv(b), w[:, k, wcol0:wcol0 + 128], rhs_list[k], start=(k == 0), stop=(k == 7)),
                           reads=rkeys + wkeys(wkey, wcol0, wcol0 + 128), writes=bk(*b))

            def ld_c2(j):
                s = j % 2
                sc.dma("sp", uT[s][:], uT_scr[j], reads=[("uT_scr", j)], writes=[("uTd", s)])
                sc.dma("sp", ya[s][:], ya_scr[j], reads=[("ya_scr", j)], writes=[("yad", s)])
                sc.dma("sp", yc[s][:], yc_scr[j], reads=[("yc_scr", j)], writes=[("ycd", s)])

            tcnt = [0]
            ocnt = [0]
            ld_c2(0)
            ctr = [0]

            def load_w2(dst, key, src, c0):
                s_ = ctr[0] % 2
                ctr[0] += 1
                sc.dma("sp", stg[s_][:], src[:, c0:c0 + 256].rearrange("(k p) c -> p k c", p=128), writes=[("stgD", s_)])
                if s_ == 0:
                    sc.add("act", ACTV(dst[:, :, c0:c0 + 256], stg[s_][:], AF.Copy), reads=[("stgD", s_)],
                           writes=[(key, c0 // 256)])
                else:
                    sc.add("dve", CP(dst[:, :, c0:c0 + 256], stg[s_][:]), reads=[("stgD", s_)], writes=[(key, c0 // 256)])
            wq_c2 = []
            for pr in range(4):
                wq_c2 += [(wg, "wg", w_gate, pr * 256), (wg, "wg", w_gate, 1024 + pr * 256),
                          (wb0, "wb0", w_branch[0], pr * 256), (wb1, "wb1", w_branch[1], pr * 256)]
            for pr in range(4):
                wq_c2.append((wo, "wo", w_out, pr * 256))

            def pump_w2(n):
                for _ in range(n):
                    if wq_c2:
                        load_w2(*wq_c2.pop(0))
            pump_w2(6)

            for j in range(8):
                s = j % 2
                if j + 1 < 8:
                    ld_c2(j + 1)
                ul = [uT[s][:, k, :] for k in range(8)]
                yal = [ya[s][:, k, :] for k in range(8)]
                ycl = [yc[s][:, k, :] for k in range(8)]
                for dmb in range(8):
                    i = tcnt[0] % 2
                    tcnt[0] += 1
                    if j == 0:
                        pump_w2(2)
                    bga = next_bank()
                    mmw(bga, wg, "wg", dmb * 128, ul, [("uTd", s)])
                    bgc = next_bank()
                    mmw(bgc, wg, "wg", 1024 + dmb * 128, ul, [("uTd", s)])
                    bpa = next_bank()
                    mmw(bpa, wb0, "wb0", dmb * 128, yal, [("yad", s)])
                    bpc = next_bank()
                    mmw(bpc, wb1, "wb1", dmb * 128, ycl, [("ycd", s)])
                    sc.add("act", ACTV(ga[i][:], pv(bga), AF.Sigmoid, bias=bgate[:, dmb:dmb + 1]),
                           reads=["bgate"], writes=bk(*bga) + [("ga", i)])
                    sc.add("act", ACTV(gc[i][:], pv(bgc), AF.Sigmoid, bias=bgate[:, 8 + dmb:9 + dmb]),
                           reads=["bgate"], writes=bk(*bgc) + [("gc", i)])
                    sc.add("dve", TT(t1[i][:], pv(bpa), ga[i][:], ALU.mult), reads=[("ga", i)],
                           writes=bk(*bpa) + [("t1", i)])
                    sc.add("dve", TT(t2[i][:], pv(bpc), gc[i][:], ALU.mult), reads=[("gc", i)],
                           writes=bk(*bpc) + [("t2d", i)])
                    sc.add("pool", TT(mT[:, dmb, :], t1[i][:], t2[i][:], ALU.add), reads=[("t1", i), ("t2d", i)],
                           writes=[("mT", dmb)])
                for a in range(4):
                    o = ocnt[0] % 2
                    ocnt[0] += 1
                    sc.dma("act", xo[o][:], x_own[j * 512 + a * 128: j * 512 + (a + 1) * 128, :], writes=[("xo", o)])
                    for cn in range(2):
                        b = next_bank()
                        for k in range(8):
                            sc.add("pe", MM(pv(b), mT[:, k, a * 128:(a + 1) * 128], wo[:, k, cn * 512:(cn + 1) * 512],
                                            start=(k == 0), stop=(k == 7)),
                                   reads=[("mT", k)] + wkeys("wo", cn * 512, (cn + 1) * 512), writes=bk(*b))
                        cs = slice(cn * 512, (cn + 1) * 512)
                        sc.add("dve", TT(rr[o][:, cs], pv(b), G1[:, cs], ALU.mult), reads=["G1"],
                               writes=bk(*b) + [("rr", o, cn)])
                        sc.add("dve", STT(rr[o][:, cs], xo[o][:, cs], ALPHA, rr[o][:, cs], ALU.mult, ALU.add),
                               reads=[("xo", o)], writes=[("rr", o, cn)])
                        sc.add("dve", (lambda o_, i_: lambda e: e.bn_stats(o_, i_))(st2[o][:, cn * 6:(cn + 1) * 6], rr[o][:, cs]),
                               reads=[("rr", o, cn)], writes=[("st2", o, cn)])
                    sc.add("dve", (lambda o_, i_: lambda e: e.bn_aggr(o_, i_))(mv2[o][:], st2[o][:]),
                           reads=[("st2", o, 0), ("st2", o, 1)], writes=[("mv2", o)])
                    sc.add("dve", TS(rs2[o][:], mv2[o][:, 1:2], LN_EPS, None, ALU.add), reads=[("mv2", o)],
                           writes=[("rs2", o)])
                    sc.add("pool", TT(rs2[o][:], rs2[o][:], mhalf[:, 0:1], ALU.pow), reads=["mhalf"], writes=[("rs2", o)])
                    sc.add("dve", TS(yy[o][:], rr[o][:], mv2[o][:, 0:1], rs2[o][:, 0:1], ALU.subtract, ALU.mult),
                           reads=[("mv2", o), ("rs2", o)], writes=[("rr", o, 0), ("rr", o, 1), ("yy", o)])
                    sc.add("pool", TT(yy[o][:], yy[o][:], lg_rep[:], ALU.mult), reads=["lg_rep"], writes=[("yy", o), ("rr", o, 0), ("rr", o, 1)])
                    sc.add("pool", TT(yy[o][:], yy[o][:], lb_rep[:], ALU.add), reads=["lb_rep"], writes=[("yy", o), ("rr", o, 0), ("rr", o, 1)])
                    sc.dma("sp", out[j * 512 + a * 128: j * 512 + (a + 1) * 128, :], yy[o][:],
                           reads=[("yy", o), ("rr", o, 0), ("rr", o, 1)], writes=[("out", j, a)])
            sc.run("pC2")
    return nc


def _bucket(n):
    if n < 16:
        return n
    v = np.log(np.float32(n) / np.float32(16.0)) / np.float32(math.log(128 / 16)) * np.float32(16.0)
    return min(16 + int(np.float32(v)), 31)


def make_in_maps(inp):
    x = np.asarray(inp["x"], np.float32)
    c = np.asarray(inp["c"], np.float32)
    f = lambda k: np.ascontiguousarray(np.asarray(inp[k], np.float32))
    w_ada = f("w_ada")[0]
    b_ada = f("b_ada")[0]
    w_in = f("w_in")[0]
    rel_bias = f("rel_bias")
    conv_w = f("conv_w")[0]
    w_branch = f("w_branch")[0]
    w_gate = f("w_gate")[0]
    b_gate = f("b_gate")[0]
    w_out = f("w_out")[0]
    ln_gain = f("ln_gain")[0]
    ln_bias = f("ln_bias")[0]
    lamv1 = np.stack([f("lambda_q1")[0], f("lambda_k1")[0], f("lambda_q2")[0], f("lambda_k2")[0]], 0)
    rep = lambda v, n=128: np.ascontiguousarray(np.broadcast_to(v[None], (n,) + v.shape))
    shared = {
        "w_ada": w_ada, "b_ada_col": np.ascontiguousarray(b_ada.reshape(24, 128).T),
        "b_ada_g": rep(b_ada[2048:3072]), "w_in": w_in, "lamv": rep(lamv1),
        "gain_col": np.ascontiguousarray(f("subln_gain")[0][:, None]),
        "rel_bias": rel_bias, "rb31": np.ascontiguousarray(rel_bias[31][:, None]),
        "conv_col": np.ascontiguousarray(conv_w.T.reshape(8, 128, 3).transpose(1, 0, 2)),
        "w_branch": w_branch, "w_gate": w_gate,
        "b_gate_col": np.ascontiguousarray(b_gate.reshape(16, 128).T), "w_out": w_out,
        "ln_gain_rep": rep(ln_gain), "ln_bias_rep": rep(ln_bias),
        "ident_in": np.eye(128, dtype=np.float32),
    }
    a_augs = []
    for half in range(2):
        dmin = 512 * half - 896
        a = np.zeros((33, NF), np.float32)
        for i in range(NF):
            n = i + dmin - 127
            if n < 0:
                a[32, i] = 1.0
            else:
                a[_bucket(n), i] = 1.0
        a_augs.append(a)
    maps = []
    for core in range(NCORES):
        b, half = core // 2, core % 2
        xb = x[b]
        x_kv = np.ascontiguousarray(xb.reshape(64, 128, D)[:, ::-1, :].reshape(S, D))
        gs = [2 * j + half for j in range(8)]
        x_own = np.ascontiguousarray(np.concatenate([xb[g * 512:(g + 1) * 512] for g in gs], 0))
        x_halo = np.zeros((16, D), np.float32)
        hm = np.ones((128, 16), np.float32)
        for j, g in enumerate(gs):
            if g == 0:
                hm[:, 2 * j:2 * j + 2] = 0.0
            else:
                x_halo[2 * j:2 * j + 2] = xb[g * 512 - 2:g * 512]
        ccol = np.ascontiguousarray(c[b].reshape(8, 128).T)
        m = dict(shared)
        m.update({
            "x_kv": x_kv, "x_own": x_own, "x_halo": x_halo, "halo_mask": hm, "c_col": ccol,
            "c_rep": np.ascontiguousarray(np.broadcast_to(ccol[:, :, None], (128, 8, 128))),
            "a_aug": a_augs[half],
        })
        maps.append(m)
    return maps


_NC_CACHE = {}


def kernel(**inputs):
    if "nc" not in _NC_CACHE:
        _NC_CACHE["nc"] = build_nc()
    nc = _NC_CACHE["nc"]
    maps = make_in_maps(inputs)
    res = run_bass_kernel_spmd(nc, maps, core_ids=list(range(NCORES)))
    outp = np.empty((4, S, D), np.float32)
    for core in range(NCORES):
        b, half = core // 2, core % 2
        o = np.asarray(res.results[core]["out"]).reshape(4096, D)
        for j in range(8):
            g = 2 * j + half
            outp[b, g * 512:(g + 1) * 512] = o[j * 512:(j + 1) * 512]
    return outp
```

```python
import math
from contextlib import ExitStack

import numpy as np
import concourse.bass as bass
import concourse.mybir as mybir
from concourse.bass_utils import run_bass_kernel_spmd

F32 = mybir.dt.float32
BF16 = mybir.dt.bfloat16
ALU = mybir.AluOpType
AF = mybir.ActivationFunctionType
AX = mybir.AxisListType

D = 1024
S = 8192
NCORES = 8
LN_EPS = 1e-5
RMS_EPS = 1e-5
LAMBDA_INIT = 0.8 - 0.6 * math.exp(0.0)
ALPHA = 2.0 ** 0.25
NF = 2048
TPW = 1536
COLTILE = False
QUAD = True


class _Op:
    __slots__ = ("eng", "fn", "dma", "deps", "need", "sem", "val", "prev", "idx", "phase")


class Sched:
    NP = 12

    def __init__(self, nc, es):
        self.nc = nc
        self.es = es
        self.sem = {}
        self.cnt = {}
        self.dpool = {}
        self.dval = {}
        self.dcnt = {}
        self.lastw = {}
        self.readers = {}
        self.waited = {}
        self.ops = []
        self.n = 0
        self.phase = 0

    def add(self, eng, fn, reads=(), writes=(), dma=False):
        op = _Op()
        op.eng, op.fn, op.dma = eng, fn, dma
        op.need, op.sem, op.val, op.prev = False, None, None, 0
        op.idx = self.n
        op.phase = self.phase
        self.n += 1
        cand = []
        wset = set(writes)
        for k in reads:
            if k in wset:
                continue
            w = self.lastw.get(k)
            if w is not None:
                cand.append(w)
        for k in wset:
            w = self.lastw.get(k)
            if w is not None:
                cand.append(w)
            cand.extend(self.readers.get(k, ()))
        latest = {}
        deps = []
        seen = set()
        for d in cand:
            if id(d) in seen:
                continue
            seen.add(id(d))
            if d.dma:
                deps.append(d)
                continue
            if d.eng == "pe" and eng == "pe" and not dma:
                continue
            if d.phase != self.phase:
                continue
            cur = latest.get(d.eng)
            if cur is None or d.idx > cur.idx:
                latest[d.eng] = d
        for d in latest.values():
            d.need = True
            deps.append(d)
        op.deps = deps
        for k in wset:
            self.lastw[k] = op
            self.readers[k] = []
        for k in reads:
            if k not in wset:
                self.readers.setdefault(k, []).append(op)
        self.ops.append(op)
        return op

    def dma(self, eng, out, in_, reads=(), writes=()):
        return self.add(eng, lambda e: e.dma_start(out=out, in_=in_), reads, writes, dma=True)

    def _pool(self, q):
        if q not in self.dpool:
            self.dpool[q] = [self.es.enter_context(self.nc.semaphore(f"d_{q}_{i}")) for i in range(self.NP)]
            self.dval[q] = [0] * self.NP
            self.dcnt[q] = 0
        return self.dpool[q]

    def run(self, name):
        nc = self.nc
        ops, self.ops = self.ops, []
        for op in ops:
            if op.dma:
                pool = self._pool(op.eng)
                i = self.dcnt[op.eng]
                self.dcnt[op.eng] += 1
                slot = i % self.NP
                op.sem = pool[slot]
                op.prev = self.dval[op.eng][slot]
                self.dval[op.eng][slot] += 16
                op.val = self.dval[op.eng][slot]
            elif op.need:
                if op.eng not in self.sem:
                    self.sem[op.eng] = self.es.enter_context(nc.semaphore(f"s_{op.eng}"))
                    self.cnt[op.eng] = 0
                self.cnt[op.eng] += 1
                op.sem = self.sem[op.eng]
                op.val = self.cnt[op.eng]
        by = {}
        for op in ops:
            by.setdefault(op.eng, []).append(op)
        sched = self

        def mk(e):
            def body(eng):
                waited = sched.waited.setdefault(e, {})
                for op in by.get(e, ()):
                    w = {}
                    for d in op.deps:
                        if d.val is None:
                            continue
                        k = id(d.sem)
                        if k not in w or w[k][1] < d.val:
                            w[k] = (d.sem, d.val)
                    if op.dma and op.prev > 0:
                        k = id(op.sem)
                        if k not in w or w[k][1] < op.prev:
                            w[k] = (op.sem, op.prev)
                    for k, (sem, val) in w.items():
                        if waited.get(k, 0) < val:
                            eng.wait_ge(sem, val)
                            waited[k] = val
                    ins = op.fn(eng)
                    if op.dma:
                        ins.then_inc(op.sem, 16)
                    elif op.need:
                        ins.then_inc(op.sem, 1)
                if e in sched.dpool:
                    for slot, sem in enumerate(sched.dpool[e]):
                        v = sched.dval[e][slot]
                        k = id(sem)
                        if v > 0 and waited.get(k, 0) < v:
                            eng.wait_ge(sem, v)
                            waited[k] = v
            return body

        with nc.Block() as block:
            block.tensor(mk("pe"))
            block.scalar(mk("act"))
            block.vector(mk("dve"))
            block.gpsimd(mk("pool"))
            block.sync(mk("sp"))
        self.phase += 1


def MM(out, lhsT, rhs, start=True, stop=True):
    return lambda e: e.matmul(out, lhsT, rhs, start=start, stop=stop)


def TR(out, in_, idn):
    return lambda e: e.transpose(out, in_, idn)


def ACTV(out, in_, func, bias=None, scale=None):
    kw = {}
    if bias is not None:
        kw["bias"] = bias
    if scale is not None:
        kw["scale"] = scale
    return lambda e: e.activation(out, in_, func, **kw)


def TT(out, in0, in1, op):
    return lambda e: e.tensor_tensor(out, in0, in1, op)


def TS(out, in0, s1, s2, op0, op1=None):
    if op1 is None:
        return lambda e: e.tensor_scalar(out, in0, s1, None, op0)
    return lambda e: e.tensor_scalar(out, in0, s1, s2, op0, op1)


def STT(out, in0, scalar, in1, op0, op1):
    return lambda e: e.scalar_tensor_tensor(out, in0, scalar, in1, op0, op1)


def CP(out, in_):
    return lambda e: e.tensor_copy(out, in_)


def MSET(out, v):
    return lambda e: e.memset(out, v)


def build_nc(stop_after=None):
    nc = bass.Bass("TRN2", target_bir_lowering=False)

    def din(name, shape, dt=F32):
        return nc.dram_tensor(name, list(shape), dt, kind="ExternalInput").ap()

    x_kv = din("x_kv", [S, D])
    x_own = din("x_own", [4096, D])
    x_halo = din("x_halo", [16, D])
    halo_mask = din("halo_mask", [128, 16])
    c_col = din("c_col", [128, 8])
    c_rep = din("c_rep", [128, 8, 128])
    w_ada = din("w_ada", [D, 3 * D])
    b_ada_col = din("b_ada_col", [128, 24])
    b_ada_g = din("b_ada_g", [128, D])
    w_in = din("w_in", [D, 8 * D])
    lamv = din("lamv", [128, 4, 64])
    gain_col = din("gain_col", [128, 1])
    rel_bias = din("rel_bias", [32, 8])
    rb31 = din("rb31", [8, 1])
    a_aug = din("a_aug", [33, NF])
    conv_col = din("conv_col", [128, 8, 3])
    w_branch = din("w_branch", [2, D, D])
    w_gate = din("w_gate", [D, 2 * D])
    b_gate_col = din("b_gate_col", [128, 16])
    w_out = din("w_out", [D, D])
    ln_gain_rep = din("ln_gain_rep", [128, D])
    ln_bias_rep = din("ln_bias_rep", [128, D])
    ident_in = din("ident_in", [128, 128])
    out = nc.dram_tensor("out", [4096, D], F32, kind="ExternalOutput").ap()

    dbg = stop_after is not None
    skind = "ExternalOutput" if dbg else "Internal"
    kT_scr = nc.dram_tensor("kT_scr", [8, 128, S], BF16, kind=skind).ap()
    v_scr = nc.dram_tensor("v_scr", [8, 128, 64, 128], BF16, kind=skind).ap()
    qT_scr = nc.dram_tensor("qT_scr", [8, 128, 4096], BF16, kind=skind).ap()
    uT_scr = nc.dram_tensor("uT_scr", [8, 128, 8, 512], BF16, kind=skind).ap()
    o_scr = nc.dram_tensor("o_scr", [8, 128, 8, 512], BF16, kind=skind).ap()
    ya_scr = nc.dram_tensor("ya_scr", [8, 128, 8, 512], BF16, kind=skind).ap()
    yc_scr = nc.dram_tensor("yc_scr", [8, 128, 8, 512], BF16, kind=skind).ap()
    fp_scr_t = nc.dram_tensor("fp_scr", [8, NF], F32, kind=skind)
    fp_scr = fp_scr_t.ap()
    dbg_t = nc.dram_tensor("dbg", [128, 2048], F32, kind=skind).ap()

    with ExitStack() as es:
        sc = Sched(nc, es)

        def sb(name, shape, dt=F32):
            return es.enter_context(nc.sbuf_tensor(name, list(shape), dt))

        PS = [es.enter_context(nc.psum_tensor(f"ps{i}", [128, 1024], F32)) for i in range(4)]
        PSB = [p.bitcast(BF16) for p in PS]

        def bk(i, h=None):
            if h is None:
                return [("ps", i, 0), ("ps", i, 1)]
            return [("ps", i, h)]

        def pv(b, n=512):
            return PS[b[0]][:, b[1] * 512: b[1] * 512 + n]

        psrot = [0]

        def next_bank():
            i = psrot[0] % 8
            psrot[0] += 1
            return (i // 2, i % 2)

        ident_f = sb("ident_f", [128, 128])
        ident = sb("ident", [128, 128], BF16)
        ones_bf = sb("ones_bf", [128, 128], BF16)
        ones_f = sb("ones_f", [128, 128])
        mhalf = sb("mhalf", [128, 512])
        modc = sb("modc", [128, 24])
        scale1 = sb("scale1", [128, 8])
        G1 = sb("G1", [128, D])
        lg_rep = sb("lg_rep", [128, D])
        lb_rep = sb("lb_rep", [128, D])
        neglam = sb("neglam", [128, 1])
        gain08 = sb("gain08", [128, 1])
        uTh = sb("uTh", [128, 8, 16], BF16)
        hmask = sb("hmask", [128, 16])
        convw = sb("convw", [128, 8, 3])
        bgate = sb("bgate", [128, 16])

        with ExitStack() as p0:
            def sb0(name, shape, dt=F32):
                return p0.enter_context(nc.sbuf_tensor(name, list(shape), dt))
            wada = sb0("wada", [128, 8, 3 * D])
            ccol = sb0("ccol", [128, 8])
            crep = sb0("crep", [128, 8, 128])
            bac = sb0("bac", [128, 24])
            bag = sb0("bag", [128, D])
            lv = sb0("lv", [128, 4, 64])
            lt = sb0("lt", [128, 2, 64])
            ls = sb0("ls", [128, 2])
            le = sb0("le", [128, 2])
            gcol = sb0("gcol", [128, 1])
            rba = sb0("rba", [33, 8])
            aug = sb0("aug", [33, NF])
            nrb = sb0("nrb", [8, 1])
            fps = sb0("fps", [8, NF])

            sc.dma("sp", ident_f[:], ident_in, writes=["ident_f"])
            sc.dma("sp", ccol[:], c_col, writes=["ccol"])
            sc.dma("sp", crep[:], c_rep, writes=["crep"])
            sc.dma("sp", bac[:], b_ada_col, writes=["bac"])
            sc.dma("sp", bag[:], b_ada_g, writes=["bag"])
            sc.dma("sp", lv[:], lamv, writes=["lv"])
            sc.dma("sp", gcol[:], gain_col, writes=["gcol"])
            sc.dma("sp", rba[0:32, :], rel_bias, writes=["rba0"])
            sc.dma("sp", nrb[:], rb31, writes=["nrb"])
            sc.dma("sp", aug[:], a_aug, writes=["aug"])
            sc.dma("sp", hmask[:], halo_mask, writes=["hmask"])
            sc.dma("sp", convw[:], conv_col, writes=["convw"])
            sc.dma("sp", bgate[:], b_gate_col, writes=["bgate"])
            sc.dma("sp", lg_rep[:], ln_gain_rep, writes=["lg_rep"])
            sc.dma("sp", lb_rep[:], ln_bias_rep, writes=["lb_rep"])
            wv_ = w_ada.rearrange("(k p) c -> p k c", p=128)
            for k in range(8):
                sc.dma("sp", wada[:, k, :], wv_[:, k, :], writes=[("wada", k)])

            sc.add("dve", CP(ident[:], ident_f[:]), reads=["ident_f"], writes=["ident"])
            sc.add("dve", MSET(ones_bf[:], 1.0), writes=["ones_bf"])
            sc.add("dve", MSET(ones_f[:], 1.0), writes=["ones_f"])
            sc.add("dve", MSET(mhalf[:], -0.5), writes=["mhalf"])
            sc.add("dve", MSET(rba[32:33, :], -30000.0), writes=["rba1"])
            sc.add("dve", TS(gain08[:], gcol[:], 1.0 - LAMBDA_INIT, None, ALU.mult), reads=["gcol"], writes=["gain08"])
            sc.add("dve", TS(nrb[:], nrb[:], -1.0, None, ALU.mult), writes=["nrb"])
            sc.add("dve", TT(lt[:, 0, :], lv[:, 0, :], lv[:, 1, :], ALU.mult), reads=["lv"], writes=["lt0"])
            sc.add("dve", TT(lt[:, 1, :], lv[:, 2, :], lv[:, 3, :], ALU.mult), reads=["lv"], writes=["lt1"])
            sc.add("dve", lambda e: e.reduce_sum(ls[:, 0:1], lt[:, 0, :], AX.X), reads=["lt0"], writes=["ls0"])
            sc.add("dve", lambda e: e.reduce_sum(ls[:, 1:2], lt[:, 1, :], AX.X), reads=["lt1"], writes=["ls1"])
            sc.add("act", ACTV(le[:], ls[:], AF.Exp), reads=["ls0", "ls1"], writes=["le"])
            sc.add("dve", STT(neglam[:], le[:, 1:2], -LAMBDA_INIT, le[:, 0:1], ALU.add, ALU.subtract),
                   reads=["le"], writes=["neglam"])
            for blk in range(24):
                for k in range(8):
                    sc.add("pe", MM(PS[0][:, blk:blk + 1], wada[:, k, blk * 128:(blk + 1) * 128], ccol[:, k:k + 1],
                                    start=(k == 0), stop=(k == 7)),
                           reads=[("wada", k), "ccol"], writes=bk(0, 0))
            sc.add("dve", TT(modc[:], PS[0][:, 0:24], bac[:], ALU.add), reads=["bac"], writes=bk(0, 0) + ["modc"])
            sc.add("dve", TS(scale1[:], modc[:, 8:16], 1.0, None, ALU.add), reads=["modc"], writes=["scale1"])
            for hf in range(2):
                for k in range(8):
                    sc.add("pe", MM(PS[1][:, hf * 512:(hf + 1) * 512], crep[:, k, :],
                                    wada[:, k, 2048 + hf * 512:2048 + (hf + 1) * 512], start=(k == 0), stop=(k == 7)),
                           reads=[("wada", k), "crep"], writes=bk(1, hf))
            sc.add("dve", TT(G1[:], PS[1][:], bag[:], ALU.add), reads=["bag"], writes=bk(1) + ["G1"])
            sc.add("dve", TS(G1[:], G1[:], 1.0, None, ALU.add), writes=["G1"])
            for cc in range(4):
                sc.add("pe", MM(PS[2 + cc // 2][0:8, (cc % 2) * 512:(cc % 2 + 1) * 512], rba[:, :],
                                aug[:, cc * 512:(cc + 1) * 512]),
                       reads=["rba0", "rba1", "aug"], writes=bk(2 + cc // 2, cc % 2))
            sc.add("act", ACTV(fps[:, 0:1024], PS[2][0:8, :], AF.Exp, bias=nrb[:, 0:1]),
                   reads=["nrb"], writes=bk(2) + ["fps0"])
            sc.add("act", ACTV(fps[:, 1024:2048], PS[3][0:8, :], AF.Exp, bias=nrb[:, 0:1]),
                   reads=["nrb"], writes=bk(3) + ["fps1"])
            sc.dma("sp", fp_scr, fps[:], reads=["fps0", "fps1"], writes=["fp_scr"])
            if stop_after == 0:
                sc.dma("sp", dbg_t[:, 0:24], modc[:], reads=["modc"], writes=["dbg0"])
                sc.dma("sp", dbg_t[:, 24:32], scale1[:], reads=["scale1"], writes=["dbg1"])
                sc.dma("sp", dbg_t[:, 1024:2048], G1[:], reads=["G1"], writes=["dbg3"])
            sc.run("p0")
        if stop_after == 0:
            return nc

        def load_weight_bf16(dst, dst_key, src_cols, ncols, stg, stg_key, slot_ctr, colblk=256):
            for c0 in range(0, ncols, colblk):
                s = slot_ctr[0] % 2
                slot_ctr[0] += 1
                src = src_cols(c0, c0 + colblk).rearrange("(k p) c -> p k c", p=128)
                sc.dma("sp", stg[s][:, :, 0:colblk], src, writes=[(stg_key, s)])
                if s == 0:
                    sc.add("act", ACTV(dst[:, :, c0:c0 + colblk], stg[s][:, :, 0:colblk], AF.Copy),
                           reads=[(stg_key, s)], writes=[(dst_key, c0 // colblk)])
                else:
                    sc.add("dve", CP(dst[:, :, c0:c0 + colblk], stg[s][:, :, 0:colblk]),
                           reads=[(stg_key, s)], writes=[(dst_key, c0 // colblk)])

        def wkeys(dst_key, c0, c1, colblk=256):
            return [(dst_key, i) for i in range(c0 // colblk, (c1 - 1) // colblk + 1)]

        ev = [0]

        def evac(out_ap, in_ap, reads, writes, scale=None, bias=None):
            ev[0] += 1
            if ev[0] % 2 == 0:
                if scale is None:
                    sc.add("dve", CP(out_ap, in_ap), reads=reads, writes=writes)
                elif bias is None:
                    sc.add("dve", TS(out_ap, in_ap, scale, None, ALU.mult), reads=reads, writes=writes)
                else:
                    sc.add("dve", TS(out_ap, in_ap, scale, bias, ALU.mult, ALU.add), reads=reads, writes=writes)
            else:
                if scale is None:
                    sc.add("act", ACTV(out_ap, in_ap, AF.Copy), reads=reads, writes=writes)
                elif bias is None:
                    sc.add("act", ACTV(out_ap, in_ap, AF.Copy, scale=scale), reads=reads, writes=writes)
                else:
                    sc.add("act", ACTV(out_ap, in_ap, AF.Identity, bias=bias, scale=scale), reads=reads, writes=writes)

        with ExitStack() as pa:
            def sba(name, shape, dt=F32):
                return pa.enter_context(nc.sbuf_tensor(name, list(shape), dt))
            wq = sba("wq", [128, 8, 1024], BF16)
            wk = sba("wk", [128, 8, 1024], BF16)
            wv = sba("wv", [128, 8, 1024], BF16)
            stg = [sba(f"stgA{i}", [128, 8, 256]) for i in range(2)]
            xt = [sba(f"xt{i}", [128, 4, D]) for i in range(2)]
            xn = [sba(f"xn{i}", [128, 4, D], BF16) for i in range(2)]
            uT = [sba(f"uT{i}", [128, 8, 512], BF16) for i in range(2)]
            ksb = [sba(f"ksb{i}", [128, 8, 512], BF16) for i in range(2)]
            vsb = [sba(f"vsb{i}", [128, 4, D], BF16) for i in range(2)]
            stats = [sba(f"stats{i}", [128, 4, 12]) for i in range(2)]
            mv = [sba(f"mv{i}", [128, 4, 2]) for i in range(2)]
            rstd = [sba(f"rstd{i}", [128, 4]) for i in range(2)]
            xh = sba("xh", [16, D])
            xhn = sba("xhn", [16, D], BF16)
            sth = sba("sth", [16, 12])
            mvh = sba("mvh", [16, 2])
            rsh = sba("rsh", [16, 1])

            def ld_x(src_rows, s):
                sc.dma("sp", xt[s][:], src_rows.rearrange("(a p) d -> p a d", p=128), writes=[("xt", s)])

            def ln_stats(s):
                for a in range(4):
                    for hh in range(2):
                        sc.add("dve", (lambda o_, i_: lambda e: e.bn_stats(o_, i_))(
                            stats[s][:, a, hh * 6:(hh + 1) * 6], xt[s][:, a, hh * 512:(hh + 1) * 512]),
                            reads=[("xt", s)], writes=[("stats", s, a, hh)])
                    sc.add("dve", (lambda o_, i_: lambda e: e.bn_aggr(o_, i_))(mv[s][:, a, :], stats[s][:, a, :]),
                           reads=[("stats", s, a, 0), ("stats", s, a, 1)], writes=[("mv", s, a)])
                sc.add("dve", TS(rstd[s][:], mv[s][:, :, 1], LN_EPS, None, ALU.add),
                       reads=[("mv", s, a) for a in range(4)], writes=[("rstd", s)])
                sc.add("pool", TT(rstd[s][:], rstd[s][:], mhalf[:, 0:4], ALU.pow), reads=["mhalf"], writes=[("rstd", s)])
                for a in range(4):
                    sc.add("dve", TS(xn[s][:, a, :], xt[s][:, a, :], mv[s][:, a, 0:1], rstd[s][:, a:a + 1],
                                     ALU.subtract, ALU.mult),
                           reads=[("xt", s), ("mv", s, a), ("rstd", s)], writes=[("xn", s, a)])

            def ln_uT(s):
                for dmc in range(8):
                    b = next_bank()
                    for a in range(4):
                        sc.add("pe", TR(PSB[b[0]][:, b[1] * 1024 + a * 128: b[1] * 1024 + (a + 1) * 128],
                                        xn[s][:, a, dmc * 128:(dmc + 1) * 128], ident[:]),
                               reads=[("xn", s, a), "ident"], writes=bk(*b))
                    evac(uT[s][:, dmc, :], PSB[b[0]][:, b[1] * 1024: b[1] * 1024 + 512],
                         reads=["scale1", "modc"], writes=bk(*b) + [("uT", s, dmc)],
                         scale=scale1[:, dmc:dmc + 1], bias=modc[:, dmc:dmc + 1])

            def proj_fm(w, wkey, s, dst, dst_key, scale=None):
                for m in range(8):
                    pump_wa(1)
                    b = next_bank()
                    for k in range(8):
                        sc.add("pe", MM(pv(b), w[:, k, m * 128:(m + 1) * 128], uT[s][:, k, :],
                                        start=(k == 0), stop=(k == 7)),
                               reads=[("uT", s, k)] + wkeys(wkey, m * 128, (m + 1) * 128), writes=bk(*b))
                    evac(dst[s][:, m, :], pv(b), reads=[], writes=bk(*b) + [(dst_key, s, m)], scale=scale)

            srcs = [x_kv[t * 512:(t + 1) * 512, :] for t in range(16)] + [x_own[j * 512:(j + 1) * 512, :] for j in range(8)]
            ld_x(srcs[0], 0)
            ctr = [0]
            wq_a = ([(wk, "wk", 1024, c0) for c0 in range(0, 1024, 256)] + [(wv, "wv", 2048, c0) for c0 in range(0, 1024, 256)]
                    + [(wq, "wq", 0, c0) for c0 in range(0, 1024, 256)])

            def pump_wa(n):
                for _ in range(n):
                    if wq_a:
                        dst_, key_, base_, c0_ = wq_a.pop(0)
                        s_ = ctr[0] % 2
                        ctr[0] += 1
                        sc.dma("sp", stg[s_][:], w_in[:, base_ + c0_:base_ + c0_ + 256].rearrange("(k p) c -> p k c", p=128),
                               writes=[("stgA", s_)])
                        if s_ == 0:
                            sc.add("act", ACTV(dst_[:, :, c0_:c0_ + 256], stg[s_][:], AF.Copy), reads=[("stgA", s_)],
                                   writes=[(key_, c0_ // 256)])
                        else:
                            sc.add("dve", CP(dst_[:, :, c0_:c0_ + 256], stg[s_][:]), reads=[("stgA", s_)],
                                   writes=[(key_, c0_ // 256)])
            pump_wa(2)
            ld_x(srcs[1], 1)
            ln_stats(0)
            for it in range(24):
                s = it % 2
                ln_uT(s)
                if it + 1 < 24:
                    ln_stats((it + 1) % 2)
                if it + 2 < 24:
                    ld_x(srcs[it + 2], s)
                if it < 16:
                    t = it
                    proj_fm(wk, "wk", s, ksb, "ksb")
                    sc.dma("sp", kT_scr[:, :, t * 512:(t + 1) * 512].rearrange("m p n -> p m n"), ksb[s][:],
                           reads=[("ksb", s, m) for m in range(8)], writes=[("kT_scr", t)])
                    for a in range(4):
                        for cn in range(2):
                            b = next_bank()
                            for k in range(8):
                                sc.add("pe", MM(pv(b), uT[s][:, k, a * 128:(a + 1) * 128],
                                                wv[:, k, cn * 512:(cn + 1) * 512], start=(k == 0), stop=(k == 7)),
                                       reads=[("uT", s, k)] + wkeys("wv", cn * 512, (cn + 1) * 512), writes=bk(*b))
                            evac(vsb[s][:, a, cn * 512:(cn + 1) * 512], pv(b), reads=[],
                                 writes=bk(*b) + [("vsb", s, a, cn)])
                        sc.dma("sp", v_scr[:, :, t * 4 + a, :].rearrange("h p d -> p h d"),
                               vsb[s][:, a, :].rearrange("p (h d) -> p h d", h=8),
                               reads=[("vsb", s, a, cn) for cn in range(2)], writes=[("v_scr", t, a)])
                else:
                    j = it - 16
                    sc.dma("sp", uT_scr[j], uT[s][:], reads=[("uT", s, d) for d in range(8)], writes=[("uT_scr", j)])
                    proj_fm(wq, "wq", s, ksb, "ksb", scale=0.125)
                    sc.dma("sp", qT_scr[:, :, j * 512:(j + 1) * 512].rearrange("m p n -> p m n"), ksb[s][:],
                           reads=[("ksb", s, m) for m in range(8)], writes=[("qT_scr", j)])
            sc.dma("sp", xh[:], x_halo, writes=["xh"])
            for hh in range(2):
                sc.add("dve", (lambda o_, i_: lambda e: e.bn_stats(o_, i_))(
                    sth[:, hh * 6:(hh + 1) * 6], xh[:, hh * 512:(hh + 1) * 512]),
                    reads=["xh"], writes=[("sth", hh)])
            sc.add("dve", lambda e: e.bn_aggr(mvh[:], sth[:]), reads=[("sth", 0), ("sth", 1)], writes=["mvh"])
            sc.add("dve", TS(rsh[:], mvh[:, 1:2], LN_EPS, None, ALU.add), reads=["mvh"], writes=["rsh"])
            sc.add("pool", TT(rsh[:], rsh[:], mhalf[0:16, 0:1], ALU.pow), reads=["mhalf"], writes=["rsh"])
            sc.add("dve", TS(xhn[:], xh[:], mvh[:, 0:1], rsh[:, 0:1], ALU.subtract, ALU.mult),
                   reads=["xh", "mvh", "rsh"], writes=["xhn"])
            for dmc in range(8):
                b = next_bank()
                sc.add("pe", TR(PSB[b[0]][:, b[1] * 1024: b[1] * 1024 + 16], xhn[:, dmc * 128:(dmc + 1) * 128],
                                ident[0:16, 0:16]),
                       reads=["xhn", "ident"], writes=bk(*b))
                evac(uTh[:, dmc, :], PSB[b[0]][:, b[1] * 1024: b[1] * 1024 + 16], reads=["scale1", "modc"],
                     writes=bk(*b) + [("uTh", dmc)], scale=scale1[:, dmc:dmc + 1], bias=modc[:, dmc:dmc + 1])
            sc.run("pA")
        if stop_after == 1:
            return nc

        with ExitStack() as pb:
            def sbb(name, shape, dt=F32):
                return pb.enter_context(nc.sbuf_tensor(name, list(shape), dt))
            KT = [sbb(f"KT{i}", [128, S], BF16) for i in range(2)]
            VT = [sbb(f"VT{i}", [128, 64, 128], BF16) for i in range(2)]
            QT = [sbb(f"QT{i}", [128, 4096], BF16) for i in range(2)]
            TP = [sbb(f"TP{i}", [128, TPW]) for i in range(2)]
            TPb = [sbb(f"TPb{i}", [128, TPW], BF16) for i in range(2)]
            NPB = 5
            PB = [sbb(f"PB{i}", [128, 1024], BF16) for i in range(NPB)]
            PF = [sbb(f"PF{i}", [128, 1024], BF16) for i in range(2)]
            OE = [sbb(f"OE{i}", [128, 1024]) for i in range(2)]
            LE = [sbb(f"LE{i}", [64, 512] if COLTILE else [128, 1024]) for i in range(2)]
            OS = [sbb(f"OS{i}", [128, 512]) for i in range(2)]
            SQ = [sbb(f"SQ{i}", [128, 512]) for i in range(2)]
            MS = sbb("MS", [128, 512])
            LL = [sbb(f"LL{i}", [128, 512]) for i in range(2)]
            ON = [sbb(f"ON{i}", [128, 512], BF16) for i in range(2)]
            lneps = sbb("lneps", [128, 1])
            sc.add("dve", MSET(lneps[:], -0.5 * math.log(RMS_EPS)), writes=["lneps"])
            sel1 = sbb("sel1", [64, 128])
            sel2 = sbb("sel2", [64, 128])
            sc.add("dve", MSET(sel1[:], 0.0), writes=["sel1"])
            sc.add("dve", MSET(sel2[:], 0.0), writes=["sel2"])
            sc.add("dve", MSET(sel1[0:1, :], 1.0), writes=["sel1"])
            sc.add("dve", MSET(sel2[32:33, :], 1.0), writes=["sel2"])

            pbc = [0]
            pfc = [0]
            onc = [0]
            qst = [0, 0]
            lpend = []
            NQS = 3
            QS = [sbb(f"QS{i}", [128, 1024], BF16) for i in range(NQS)]

            def load_head(h):
                hs = h % 2
                sc.dma("sp", KT[hs][:], kT_scr[h], reads=[("kT_scr", t) for t in range(16)], writes=[("KT", hs)])
                sc.dma("sp", VT[hs][:], v_scr[h], reads=[("v_scr", t, a) for t in range(16) for a in range(4)],
                       writes=[("VT", hs)])
                sc.dma("sp", QT[hs][:], qT_scr[h], reads=[("qT_scr", j) for j in range(8)], writes=[("QT", hs)])
                sc.dma("sp", TP[hs][:], bass.AP(fp_scr_t, h * NF, [[1, 128], [1, TPW]]),
                       reads=["fp_scr"], writes=[("TP", hs)])
                sc.add("pool", CP(TPb[hs][:], TP[hs][:]), reads=[("TP", hs)], writes=[("TPb", hs)])

            items = []
            fifo = []
            cc = 0
            for h in range(8):
                for j in range(8):
                    e = cc % 2
                    cc += 1
                    for kb in range(8 * j + 8):
                        items.append(("u", h, j, kb, e))
                        while fifo and fifo[0][0] <= len(items):
                            items.append(fifo.pop(0)[1])
                    for n_, nm in ((2, "le"), (4, "g1"), (6, "g2"), (8, "g3"), (10, "p1"), (12, "p2"), (14, "p3"),
                                   (16, "p4"), (18, "p5"), (22, "f3"), (24, "f4"), (26, "f5"), (28, "f6")):
                        fifo.append((len(items) + n_, (nm, h, j, e)))
                    fifo.sort(key=lambda t_: t_[0])
            for f_ in fifo:
                items.append(f_[1])
                items.append(("nop",))
                items.append(("nop",))

            slot_of = {}
            kcnt = 0
            for i_, it_ in enumerate(items):
                if it_[0] in ("u", "f3"):
                    slot_of[i_] = kcnt % 2
                    kcnt += 1

            def stage1(idx):
                it = items[idx]
                s = slot_of.get(idx, 0)
                if it[0] == "u":
                    _, h, j, kb, e = it
                    hs = h % 2
                    qs = slice(j * 512, (j + 1) * 512)
                    ks = slice(kb * 128, (kb + 1) * 128)
                    sc.add("pe", MM(PS[s][:, 0:512], KT[hs][0:64, ks], QT[hs][0:64, qs]),
                           reads=[("KT", hs), ("QT", hs)], writes=bk(s, 0))
                    sc.add("pe", MM(PS[s][:, 512:1024], KT[hs][64:128, ks], QT[hs][64:128, qs]),
                           reads=[("KT", hs), ("QT", hs)], writes=bk(s, 1))
                elif it[0] == "f1":
                    e = it[3]
                    pass
                elif it[0] == "f2":
                    e = it[3]
                    pass
                elif it[0] == "f3":
                    e = it[3]
                    sc.add("pe", MM(PS[s][:, 0:512], ones_f[:], SQ[e][:]), reads=[("SQ", e), "ones_f"], writes=bk(s, 0))

            ust = {}

            def stage2a(idx):
                it = items[idx]
                s = slot_of.get(idx, 0)
                if it[0] == "f3":
                    e = it[3]
                    sc.add("dve", TS(MS[:], PS[s][:, 0:512], 1.0 / 128.0, RMS_EPS, ALU.mult, ALU.add),
                           reads=[], writes=bk(s, 0) + ["MS"])
                    return
                if it[0] != "u":
                    return
                _, h, j, kb, e = it
                hs = h % 2
                if j == 4 and kb == 0 and h + 1 < 8:
                    load_head(h + 1)
                p = pbc[0] % NPB
                pbc[0] += 1
                ust[idx] = p
                pk = [("PB", p, 0), ("PB", p, 1)]
                if kb < 8 * j - 1:
                    sc.add("act", ACTV(PB[p][:], PS[s][:], AF.Exp), reads=[], writes=bk(s) + pk)
                else:
                    f = pfc[0] % 2
                    pfc[0] += 1
                    off = 128 * (8 * j + 7 - kb)
                    for m in range(2):
                        ms_ = slice(m * 512, (m + 1) * 512)
                        sc.add("act", ACTV(PF[f][:, ms_], PS[s][:, ms_], AF.Exp), reads=[],
                               writes=bk(s, m) + [("PF", f, m)])
                        sc.add("dve", TT(PB[p][:, ms_], PF[f][:, ms_], TPb[hs][:, off:off + 512], ALU.mult),
                               reads=[("PF", f, m), ("TPb", hs)], writes=[pk[m]])

            def stage2b(idx):
                it = items[idx]
                s = slot_of.get(idx, 0)
                if it[0] == "g1":
                    e = it[3]
                    sc.add("dve", TT(LL[e][:], LE[e][:, 0:512], LE[e][:, 512:1024], ALU.mult),
                           reads=[("LE", e)], writes=[("LL", e)])
                    return
                if it[0] in ("g2", "g3"):
                    e = it[3]
                    cs_ = slice(0, 256) if it[0] == "g2" else slice(256, 512)
                    sc.add("dve", (lambda o_: lambda en: en.reciprocal(o_, o_))(LL[e][:, cs_]), reads=[], writes=[("LL", e)])
                    return
                if it[0] == "p1":
                    e = it[3]
                    sc.add("dve", TT(OE[e][:, 0:512], OE[e][:, 0:512], LE[e][:, 512:1024], ALU.mult),
                           reads=[("LE", e)], writes=[("OE", e)])
                    return
                if it[0] == "p2":
                    e = it[3]
                    sc.add("dve", TT(OE[e][:, 512:1024], OE[e][:, 512:1024], LE[e][:, 0:512], ALU.mult),
                           reads=[("LE", e)], writes=[("OE", e)])
                    return
                if it[0] == "p3":
                    e = it[3]
                    sc.add("dve", TT(OS[e][:], OE[e][:, 0:512], OE[e][:, 512:1024], ALU.add),
                           reads=[("OE", e)], writes=[("OS", e)])
                    return
                if it[0] == "p4":
                    e = it[3]
                    sc.add("dve", TT(OS[e][:], OS[e][:], LL[e][:], ALU.mult), reads=[("LL", e)], writes=[("OS", e)])
                    return
                if it[0] == "p5":
                    e = it[3]
                    sc.add("dve", TT(SQ[e][:], OS[e][:], OS[e][:], ALU.mult), reads=[("OS", e)], writes=[("SQ", e)])
                    return
                if it[0] == "nop":
                    return
                if it[0] == "le":
                    e = it[3]
                    sc.add("act", ACTV(LE[e][:], PS[3][:], AF.Copy), reads=[], writes=bk(3) + [("LE", e)])
                    return
                if it[0] == "f3":
                    return
                if it[0] == "f4":
                    sc.add("act", ACTV(MS[:], MS[:], AF.Ln), reads=[], writes=["MS"])
                    return
                if it[0] == "f5":
                    sc.add("act", ACTV(MS[:], MS[:], AF.Exp, scale=-0.5), reads=[], writes=["MS"])
                    return
                if it[0] == "f6":
                    _, h, j, e = it
                    o = onc[0] % 2
                    onc[0] += 1
                    sc.add("dve", STT(ON[o][:], OS[e][:], gain08[:, 0:1], MS[:], ALU.mult, ALU.mult),
                           reads=[("OS", e), "MS", "gain08"], writes=[("ON", o)])
                    sc.dma("sp", o_scr[j][:, h, :], ON[o][:], reads=[("ON", o)], writes=[("o_scr", j, h)])
                    return
                _, h, j, kb, e = it
                hs = h % 2
                nkb = 8 * j + 8
                p = ust.pop(idx)
                pk = [("PB", p, 0), ("PB", p, 1)]
                st, sp_ = (kb == 0), (kb == nkb - 1)
                for m in range(2):
                    sc.add("pe", MM(PS[2][:, m * 512:(m + 1) * 512], VT[hs][:, kb, :],
                                    PB[p][:, m * 512:(m + 1) * 512], start=st, stop=sp_),
                           reads=[pk[m], ("VT", hs)], writes=bk(2, m))
                if kb >= 8 * j:
                    gsz, gpos, gfirst = 2, (kb - 8 * j) % 2, (j == 0 and kb < 2)
                else:
                    gsz, gpos, gfirst = 4, kb % 4, (kb < 4)
                if gpos == 0:
                    qst[0] = p
                    qst[1] += 1
                q = qst[1] % NQS

                def emit_L(q_, st_, sp__):
                    for m in range(2):
                        sc.add("pe", MM(PS[3][:, m * 512:(m + 1) * 512], ones_bf[:],
                                        QS[q_][:, m * 512:(m + 1) * 512], start=st_, stop=sp__),
                               reads=[("QS", q_), "ones_bf"], writes=bk(3, m))
                if gpos == 1 and lpend:
                    emit_L(*lpend.pop())
                if gpos == 1:
                    sc.add("dve", TT(QS[q][:], PB[qst[0]][:], PB[p][:], ALU.add),
                           reads=[("PB", qst[0], 0), ("PB", qst[0], 1)] + pk, writes=[("QS", q)])
                elif gpos > 1:
                    sc.add("dve", TT(QS[q][:], QS[q][:], PB[p][:], ALU.add), reads=pk, writes=[("QS", q)])
                if gpos == gsz - 1:
                    if sp_:
                        emit_L(q, gfirst, True)
                    else:
                        lpend.append((q, gfirst, False))
                if kb == nkb - 1:
                    sc.add("dve", CP(OE[e][:, 0:512], PS[2][:, 0:512]), reads=[], writes=bk(2, 0) + [("OE", e)])
                    sc.add("dve", TS(OE[e][:, 512:1024], PS[2][:, 512:1024], neglam[:, 0:1], None, ALU.mult),
                           reads=["neglam"], writes=bk(2, 1) + [("OE", e)])

            load_head(0)
            stage1(0)
            stage1(1)
            for idx in range(len(items)):
                stage2a(idx)
                if idx + 2 < len(items):
                    stage1(idx + 2)
                stage2b(idx)
            sc.run("pB")
        if stop_after == 2:
            return nc

        with ExitStack() as pc:
            def sbc(name, shape, dt=F32):
                return pc.enter_context(nc.sbuf_tensor(name, list(shape), dt))
            wc = sbc("wc", [128, 8, 5120], BF16)
            stg = [sbc(f"stgC{i}", [128, 8, 256]) for i in range(2)]
            uT = [sbc(f"uTc{i}", [128, 8, 512], BF16) for i in range(2)]
            oT = [sbc(f"oTc{i}", [128, 8, 512], BF16) for i in range(2)]
            ya = [sbc(f"yac{i}", [128, 8, 512], BF16) for i in range(2)]
            yc = [sbc(f"ycc{i}", [128, 8, 512], BF16) for i in range(2)]
            vh = sbc("vh", [128, 8, 16])
            za = [sbc(f"za{i}", [128, 512]) for i in range(2)]
            Csb = [sbc(f"Csb{i}", [128, 512]) for i in range(2)]
            vv = [sbc(f"vv{i}", [128, 514]) for i in range(2)]
            tt = [sbc(f"tt{i}", [128, 512]) for i in range(2)]
            sz = [sbc(f"sz{i}", [128, 512]) for i in range(2)]
            t2 = [sbc(f"t2{i}", [128, 512]) for i in range(2)]

            def mm8(b, wcol0, rhs_list, rkeys, n=512):
                for k in range(8):
                    sc.add("pe", MM(pv(b, n), wc[:, k, wcol0:wcol0 + 128], rhs_list[k], start=(k == 0), stop=(k == 7)),
                           reads=rkeys + wkeys("wc", wcol0, wcol0 + 128), writes=bk(*b))

            hk = [("uTh", d) for d in range(8)]
            hl = [uTh[:, k, :] for k in range(8)]

            def halo(cb):
                b1 = next_bank()
                mm8(b1, 2048 + cb * 128, hl, hk, n=16)
                b2 = next_bank()
                mm8(b2, 3072 + cb * 128, hl, hk, n=16)
                sc.add("dve", TT(vh[:, cb, :], pv(b1, 16), hmask[:], ALU.mult), reads=["hmask"],
                       writes=bk(*b1) + [("vh", cb)])
                sc.add("dve", TT(vh[:, cb, :], pv(b2, 16), vh[:, cb, :], ALU.mult), reads=[],
                       writes=bk(*b2) + [("vh", cb)])

            def ld_c1b(j):
                s = j % 2
                sc.dma("sp", uT[s][:], uT_scr[j], reads=[("uT_scr", j)], writes=[("uTc", s)])

            def ld_c1(j):
                s = j % 2
                sc.dma("sp", uT[s][:], uT_scr[j], reads=[("uT_scr", j)], writes=[("uTc", s)])
                sc.dma("sp", oT[s][:], o_scr[j], reads=[("o_scr", j, h) for h in range(8)], writes=[("oTc", s)])

            tc_ = [0]
            ld_c1(0)
            ctr = [0]

            def load_wc(c0):
                s_ = ctr[0] % 2
                ctr[0] += 1
                src = w_in[:, 3072 + c0:3072 + c0 + 256].rearrange("(k p) c -> p k c", p=128)
                sc.dma("sp", stg[s_][:], src, writes=[("stgC", s_)])
                if s_ == 0:
                    sc.add("act", ACTV(wc[:, :, c0:c0 + 256], stg[s_][:], AF.Copy), reads=[("stgC", s_)],
                           writes=[("wc", c0 // 256)])
                else:
                    sc.add("dve", CP(wc[:, :, c0:c0 + 256], stg[s_][:]), reads=[("stgC", s_)], writes=[("wc", c0 // 256)])
            wq_c1 = [c0 for c0 in range(0, 1024, 256)] + [base + pr * 256 for pr in range(4)
                                                           for base in (1024, 2048, 3072, 4096)]

            def pump_wc(n):
                for _ in range(n):
                    if wq_c1:
                        load_wc(wq_c1.pop(0))
            pump_wc(2)

            for j in range(8):
                s = j % 2
                if j + 1 < 8:
                    ld_c1(j + 1)
                ukeys = [("uTc", s)]
                ul = [uT[s][:, k, :] for k in range(8)]
                for m in range(8):
                    i = tc_[0] % 2
                    tc_[0] += 1
                    if j == 0 and m % 2 == 0:
                        pump_wc(1)
                    if j > 0 and m in (0, 4):
                        pump_wc(1)
                    b = next_bank()
                    mm8(b, m * 128, ul, ukeys)
                    sc.add("act", ACTV(za[i][:], pv(b), AF.Silu), reads=[], writes=bk(*b) + [("za", i)])
                    sc.add("pool", TT(ya[s][:, m, :], za[i][:], oT[s][:, m, :], ALU.mult),
                           reads=[("za", i), ("oTc", s)], writes=[("yac", s, m)])
                sc.dma("sp", ya_scr[j], ya[s][:], reads=[("yac", s, m) for m in range(8)], writes=[("ya_scr", j)])
            pump_wc(100)
            ld_c1b(0)
            for j in range(8):
                s = j % 2
                if j + 1 < 8:
                    ld_c1b(j + 1)
                ukeys = [("uTc", s)]
                ul = [uT[s][:, k, :] for k in range(8)]
                for cb in range(8):
                    i = tc_[0] % 2
                    tc_[0] += 1
                    if j == 0:
                        halo(cb)
                    bB = next_bank()
                    mm8(bB, 1024 + cb * 128, ul, ukeys)
                    bC = next_bank()
                    mm8(bC, 2048 + cb * 128, ul, ukeys)
                    bX = next_bank()
                    mm8(bX, 3072 + cb * 128, ul, ukeys)
                    bZ = next_bank()
                    mm8(bZ, 4096 + cb * 128, ul, ukeys)
                    sc.add("act", ACTV(Csb[i][:], pv(bC), AF.Copy), reads=[], writes=bk(*bC) + [("Csb", i)])
                    sc.add("act", ACTV(sz[i][:], pv(bZ), AF.Silu), reads=[], writes=bk(*bZ) + [("sz", i)])
                    sc.add("dve", TT(vv[i][:, 2:514], pv(bX), Csb[i][:], ALU.mult), reads=[("Csb", i)],
                           writes=bk(*bX) + [("vv", i)])
                    sc.add("pool", CP(vv[i][:, 0:2], vh[:, cb, 2 * j:2 * j + 2]), reads=[("vh", cb)],
                           writes=[("vvh", i)])
                    rk = [("vv", i), ("vvh", i), "convw"]
                    sc.add("dve", TS(tt[i][:], vv[i][:, 0:512], convw[:, cb, 0:1], None, ALU.mult),
                           reads=rk, writes=[("tt", i)])
                    sc.add("dve", STT(tt[i][:], vv[i][:, 1:513], convw[:, cb, 1:2], tt[i][:], ALU.mult, ALU.add),
                           reads=rk, writes=[("tt", i)])
                    sc.add("dve", STT(tt[i][:], vv[i][:, 2:514], convw[:, cb, 2:3], tt[i][:], ALU.mult, ALU.add),
                           reads=rk, writes=[("tt", i)])
                    sc.add("dve", TT(t2[i][:], pv(bB), tt[i][:], ALU.mult), reads=[("tt", i)],
                           writes=bk(*bB) + [("t2", i)])
                    sc.add("pool", TT(yc[s][:, cb, :], t2[i][:], sz[i][:], ALU.mult), reads=[("t2", i), ("sz", i)],
                           writes=[("ycc", s, cb)])
                sc.dma("sp", yc_scr[j], yc[s][:], reads=[("ycc", s, cb) for cb in range(8)], writes=[("yc_scr", j)])
            sc.run("pC1")
        if stop_after == 3:
            return nc

        with ExitStack() as pd:
            def sbd(name, shape, dt=F32):
                return pd.enter_context(nc.sbuf_tensor(name, list(shape), dt))
            wg = sbd("wg", [128, 8, 2048], BF16)
            wb0 = sbd("wb0", [128, 8, 1024], BF16)
            wb1 = sbd("wb1", [128, 8, 1024], BF16)
            wo = sbd("wo", [128, 8, 1024], BF16)
            stg = [sbd(f"stgD{i}", [128, 8, 256]) for i in range(2)]
            uT = [sbd(f"uTd{i}", [128, 8, 512], BF16) for i in range(2)]
            ya = [sbd(f"yad{i}", [128, 8, 512], BF16) for i in range(2)]
            yc = [sbd(f"ycd{i}", [128, 8, 512], BF16) for i in range(2)]
            xo = [sbd(f"xo{i}", [128, D]) for i in range(2)]
            mT = sbd("mT", [128, 8, 512], BF16)
            ga = [sbd(f"ga{i}", [128, 512]) for i in range(2)]
            gc = [sbd(f"gc{i}", [128, 512]) for i in range(2)]
            t1 = [sbd(f"t1{i}", [128, 512]) for i in range(2)]
            t2 = [sbd(f"t2d{i}", [128, 512]) for i in range(2)]
            rr = [sbd(f"rr{i}", [128, D]) for i in range(2)]
            yy = rr
            st2 = [sbd(f"st2{i}", [128, 12]) for i in range(2)]
            mv2 = [sbd(f"mv2{i}", [128, 2]) for i in range(2)]
            rs2 = [sbd(f"rs2{i}", [128, 1]) for i in range(2)]

            def mmw(b, w, wkey, wcol0, rhs_list, rkeys):
                for k in range(8):
                    sc.add("pe", MM(pv(b), w[:, k, wcol0:wcol0 + 128], rhs_list[k], start=(k == 0), stop=(k == 7)),
                           reads=rkeys + wkeys(wkey, wcol0, wcol0 + 128), writes=bk(*b))

            def ld_c2(j):
                s = j % 2
                sc.dma("sp", uT[s][:], uT_scr[j], reads=[("uT_scr", j)], writes=[("uTd", s)])
                sc.dma("sp", ya[s][:], ya_scr[j], reads=[("ya_scr", j)], writes=[("yad", s)])
                sc.dma("sp", yc[s][:], yc_scr[j], reads=[("yc_scr", j)], writes=[("ycd", s)])

            tcnt = [0]
            ocnt = [0]
            ld_c2(0)
            ctr = [0]

            def load_w2(dst, key, src, c0):
                s_ = ctr[0] % 2
                ctr[0] += 1
                sc.dma("sp", stg[s_][:], src[:, c0:c0 + 256].rearrange("(k p) c -> p k c", p=128), writes=[("stgD", s_)])
                if s_ == 0:
                    sc.add("act", ACTV(dst[:, :, c0:c0 + 256], stg[s_][:], AF.Copy), reads=[("stgD", s_)],
                           writes=[(key, c0 // 256)])
                else:
                    sc.add("dve", CP(dst[:, :, c0:c0 + 256], stg[s_][:]), reads=[("stgD", s_)], writes=[(key, c0 // 256)])
            wq_c2 = []
            for pr in range(4):
                wq_c2 += [(wg, "wg", w_gate, pr * 256), (wg, "wg", w_gate, 1024 + pr * 256),
                          (wb0, "wb0", w_branch[0], pr * 256), (wb1, "wb1", w_branch[1], pr * 256)]
            for pr in range(4):
                wq_c2.append((wo, "wo", w_out, pr * 256))

            def pump_w2(n):
                for _ in range(n):
                    if wq_c2:
                        load_w2(*wq_c2.pop(0))
            pump_w2(6)

            for j in range(8):
                s = j % 2
                if j + 1 < 8:
                    ld_c2(j + 1)
                ul = [uT[s][:, k, :] for k in range(8)]
                yal = [ya[s][:, k, :] for k in range(8)]
                ycl = [yc[s][:, k, :] for k in range(8)]
                for dmb in range(8):
                    i = tcnt[0] % 2
                    tcnt[0] += 1
                    if j == 0:
                        pump_w2(2)
                    bga = next_bank()
                    mmw(bga, wg, "wg", dmb * 128, ul, [("uTd", s)])
                    bgc = next_bank()
                    mmw(bgc, wg, "wg", 1024 + dmb * 128, ul, [("uTd", s)])
                    bpa = next_bank()
                    mmw(bpa, wb0, "wb0", dmb * 128, yal, [("yad", s)])
                    bpc = next_bank()
                    mmw(bpc, wb1, "wb1", dmb * 128, ycl, [("ycd", s)])
                    sc.add("act", ACTV(ga[i][:], pv(bga), AF.Sigmoid, bias=bgate[:, dmb:dmb + 1]),
                           reads=["bgate"], writes=bk(*bga) + [("ga", i)])
                    sc.add("act", ACTV(gc[i][:], pv(bgc), AF.Sigmoid, bias=bgate[:, 8 + dmb:9 + dmb]),
                           reads=["bgate"], writes=bk(*bgc) + [("gc", i)])
                    sc.add("dve", TT(t1[i][:], pv(bpa), ga[i][:], ALU.mult), reads=[("ga", i)],
                           writes=bk(*bpa) + [("t1", i)])
                    sc.add("dve", TT(t2[i][:], pv(bpc), gc[i][:], ALU.mult), reads=[("gc", i)],
                           writes=bk(*bpc) + [("t2d", i)])
                    sc.add("pool", TT(mT[:, dmb, :], t1[i][:], t2[i][:], ALU.add), reads=[("t1", i), ("t2d", i)],
                           writes=[("mT", dmb)])
                for a in range(4):
                    o = ocnt[0] % 2
                    ocnt[0] += 1
                    sc.dma("act", xo[o][:], x_own[j * 512 + a * 128: j * 512 + (a + 1) * 128, :], writes=[("xo", o)])
                    for cn in range(2):
                        b = next_bank()
                        for k in range(8):
                            sc.add("pe", MM(pv(b), mT[:, k, a * 128:(a + 1) * 128], wo[:, k, cn * 512:(cn + 1) * 512],
                                            start=(k == 0), stop=(k == 7)),
                                   reads=[("mT", k)] + wkeys("wo", cn * 512, (cn + 1) * 512), writes=bk(*b))
                        cs = slice(cn * 512, (cn + 1) * 512)
                        sc.add("dve", TT(rr[o][:, cs], pv(b), G1[:, cs], ALU.mult), reads=["G1"],
                               writes=bk(*b) + [("rr", o, cn)])
                        sc.add("dve", STT(rr[o][:, cs], xo[o][:, cs], ALPHA, rr[o][:, cs], ALU.mult, ALU.add),
                               reads=[("xo", o)], writes=[("rr", o, cn)])
                        sc.add("dve", (lambda o_, i_: lambda e: e.bn_stats(o_, i_))(st2[o][:, cn * 6:(cn + 1) * 6], rr[o][:, cs]),
                               reads=[("rr", o, cn)], writes=[("st2", o, cn)])
                    sc.add("dve", (lambda o_, i_: lambda e: e.bn_aggr(o_, i_))(mv2[o][:], st2[o][:]),
                           reads=[("st2", o, 0), ("st2", o, 1)], writes=[("mv2", o)])
                    sc.add("dve", TS(rs2[o][:], mv2[o][:, 1:2], LN_EPS, None, ALU.add), reads=[("mv2", o)],
                           writes=[("rs2", o)])
                    sc.add("pool", TT(rs2[o][:], rs2[o][:], mhalf[:, 0:1], ALU.pow), reads=["mhalf"], writes=[("rs2", o)])
                    sc.add("dve", TS(yy[o][:], rr[o][:], mv2[o][:, 0:1], rs2[o][:, 0:1], ALU.subtract, ALU.mult),
                           reads=[("mv2", o), ("rs2", o)], writes=[("rr", o, 0), ("rr", o, 1), ("yy", o)])
                    sc.add("pool", TT(yy[o][:], yy[o][:], lg_rep[:], ALU.mult), reads=["lg_rep"], writes=[("yy", o), ("rr", o, 0), ("rr", o, 1)])
                    sc.add("pool", TT(yy[o][:], yy[o][:], lb_rep[:], ALU.add), reads=["lb_rep"], writes=[("yy", o), ("rr", o, 0), ("rr", o, 1)])
                    sc.dma("sp", out[j * 512 + a * 128: j * 512 + (a + 1) * 128, :], yy[o][:],
                           reads=[("yy", o), ("rr", o, 0), ("rr", o, 1)], writes=[("out", j, a)])
            sc.run("pC2")
    return nc


def _bucket(n):
    if n < 16:
        return n
    v = np.log(np.float32(n) / np.float32(16.0)) / np.float32(math.log(128 / 16)) * np.float32(16.0)
    return min(16 + int(np.float32(v)), 31)


def make_in_maps(inp):
    x = np.asarray(inp["x"], np.float32)
    c = np.asarray(inp["c"], np.float32)
    f = lambda k: np.ascontiguousarray(np.asarray(inp[k], np.float32))
    w_ada = f("w_ada")[0]
    b_ada = f("b_ada")[0]
    w_in = f("w_in")[0]
    rel_bias = f("rel_bias")
    conv_w = f("conv_w")[0]
    w_branch = f("w_branch")[0]
    w_gate = f("w_gate")[0]
    b_gate = f("b_gate")[0]
    w_out = f("w_out")[0]
    ln_gain = f("ln_gain")[0]
    ln_bias = f("ln_bias")[0]
    lamv1 = np.stack([f("lambda_q1")[0], f("lambda_k1")[0], f("lambda_q2")[0], f("lambda_k2")[0]], 0)
    rep = lambda v, n=128: np.ascontiguousarray(np.broadcast_to(v[None], (n,) + v.shape))
    shared = {
        "w_ada": w_ada, "b_ada_col": np.ascontiguousarray(b_ada.reshape(24, 128).T),
        "b_ada_g": rep(b_ada[2048:3072]), "w_in": w_in, "lamv": rep(lamv1),
        "gain_col": np.ascontiguousarray(f("subln_gain")[0][:, None]),
        "rel_bias": rel_bias, "rb31": np.ascontiguousarray(rel_bias[31][:, None]),
        "conv_col": np.ascontiguousarray(conv_w.T.reshape(8, 128, 3).transpose(1, 0, 2)),
        "w_branch": w_branch, "w_gate": w_gate,
        "b_gate_col": np.ascontiguousarray(b_gate.reshape(16, 128).T), "w_out": w_out,
        "ln_gain_rep": rep(ln_gain), "ln_bias_rep": rep(ln_bias),
        "ident_in": np.eye(128, dtype=np.float32),
    }
    a_augs = []
    for half in range(2):
        dmin = 512 * half - 896
        a = np.zeros((33, NF), np.float32)
        for i in range(NF):
            n = i + dmin - 127
            if n < 0:
                a[32, i] = 1.0
            else:
                a[_bucket(n), i] = 1.0
        a_augs.append(a)
    maps = []
    for core in range(NCORES):
        b, half = core // 2, core % 2
        xb = x[b]
        x_kv = np.ascontiguousarray(xb.reshape(64, 128, D)[:, ::-1, :].reshape(S, D))
        gs = [2 * j + half for j in range(8)]
        x_own = np.ascontiguousarray(np.concatenate([xb[g * 512:(g + 1) * 512] for g in gs], 0))
        x_halo = np.zeros((16, D), np.float32)
        hm = np.ones((128, 16), np.float32)
        for j, g in enumerate(gs):
            if g == 0:
                hm[:, 2 * j:2 * j + 2] = 0.0
            else:
                x_halo[2 * j:2 * j + 2] = xb[g * 512 - 2:g * 512]
        ccol = np.ascontiguousarray(c[b].reshape(8, 128).T)
        m = dict(shared)
        m.update({
            "x_kv": x_kv, "x_own": x_own, "x_halo": x_halo, "halo_mask": hm, "c_col": ccol,
            "c_rep": np.ascontiguousarray(np.broadcast_to(ccol[:, :, None], (128, 8, 128))),
            "a_aug": a_augs[half],
        })
        maps.append(m)
    return maps


_NC_CACHE = {}


def kernel(**inputs):
    if "nc" not in _NC_CACHE:
        _NC_CACHE["nc"] = build_nc()
    nc = _NC_CACHE["nc"]
    maps = make_in_maps(inputs)
    res = run_bass_kernel_spmd(nc, maps, core_ids=list(range(NCORES)))
    outp = np.empty((4, S, D), np.float32)
    for core in range(NCORES):
        b, half = core // 2, core % 2
        o = np.asarray(res.results[core]["out"]).reshape(4096, D)
        for j in range(8):
            g = 2 * j + half
            outp[b, g * 512:(g + 1) * 512] = o[j * 512:(j + 1) * 512]
    return outp
```
